# Optimizing a Trainium2 kernel written in Bass

```python
import math
import jax, jax.numpy as jnp
from jax import lax
import numpy as np

D_MODEL = 1024
BATCH = 4
SEQ = 8192
DEPTH = 1
DEC_BATCH = 16
DEC_SEQ = 4096
PAST_LEN = 128

HEAD_DIM = 64
A_HEADS = 8
A_WIDTH = A_HEADS * HEAD_DIM
A_GROUPS = ((128, 1), (512, 4), (2048, 16))
B_HEADS = 8
B_KV_HEADS = 2
B_GROUP = B_HEADS // B_KV_HEADS
B_WIDTH = B_HEADS * HEAD_DIM
B_KV_WIDTH = B_KV_HEADS * HEAD_DIM
B_HALF_WINDOW = 128
ROPE_THETA = 10000.0
LN_EPS = 1e-5
NEG_INF = -1e30
DEEPNORM_ALPHA = (2.0 * DEPTH) ** 0.25
DEEPNORM_BETA = (8.0 * DEPTH) ** -0.25
IN_SPLIT_SIZES = (A_WIDTH, A_WIDTH, A_WIDTH, A_WIDTH, B_WIDTH, B_KV_WIDTH, B_KV_WIDTH, B_WIDTH, D_MODEL, D_MODEL)
IN_WIDTH = sum(IN_SPLIT_SIZES)
IN_SPLIT_POINTS = tuple(int(v) for v in np.cumsum(IN_SPLIT_SIZES)[:-1])

kernel_name = "hybrid_dilated_window_encoder"


def layer_norm(z, g, b):
    zf = z.astype(jnp.float32)
    mu = zf.mean(-1, keepdims=True)
    var = jnp.mean(jnp.square(zf - mu), -1, keepdims=True)
    out = (zf - mu) * lax.rsqrt(var + LN_EPS) * g.astype(jnp.float32) + b.astype(jnp.float32)
    return out.astype(z.dtype)


def rope(x, pos):
    half = HEAD_DIM // 2
    inv_freq = ROPE_THETA ** (-jnp.arange(half, dtype=jnp.float32) / half)
    ang = pos.astype(jnp.float32)[:, None] * inv_freq[None, :]
    cos = jnp.cos(ang)[None, :, None, :]
    sin = jnp.sin(ang)[None, :, None, :]
    x1 = x[..., :half].astype(jnp.float32)
    x2 = x[..., half:].astype(jnp.float32)
    out = jnp.concatenate([x1 * cos - x2 * sin, x2 * cos + x1 * sin], axis=-1)
    return out.astype(x.dtype)


def banded_attention(q, k, v, half_window, sink=None):
    blk = half_window
    n, L = q.shape[0], q.shape[1]
    hkv, g, dh = q.shape[2], q.shape[3], q.shape[4]
    nb = -(-L // blk)
    lp = nb * blk
    pad = lp - L
    qb = jnp.pad(q, ((0, 0), (0, pad), (0, 0), (0, 0), (0, 0))).reshape(n, nb, blk, hkv, g, dh)

    def frames(t):
        t = jnp.pad(t, ((0, 0), (blk, pad + blk), (0, 0), (0, 0))).reshape(n, nb + 2, blk, hkv, dh)
        return jnp.concatenate([t[:, :-2], t[:, 1:-1], t[:, 2:]], axis=2)

    kb, vb = frames(k), frames(v)
    scale = 1.0 / math.sqrt(dh)
    s = jnp.einsum("nbqhgd,nbkhd->nbhgqk", qb, kb, preferred_element_type=jnp.float32) * scale
    qi = jnp.arange(nb)[:, None] * blk + jnp.arange(blk)[None, :]
    kj = (jnp.arange(nb)[:, None] - 1) * blk + jnp.arange(3 * blk)[None, :]
    valid = (jnp.abs(qi[:, :, None] - kj[:, None, :]) <= half_window) & (kj[:, None, :] >= 0) & (kj[:, None, :] < L)
    s = jnp.where(valid[None, :, None, None, :, :], s, NEG_INF)
    m = s.max(-1)
    if sink is not None:
        sink_f = sink.astype(jnp.float32)[None, None, :, :, None]
        m = jnp.maximum(m, sink_f)
    p = jnp.exp(s - m[..., None])
    denom = p.sum(-1)
    if sink is not None:
        denom = denom + jnp.exp(sink_f - m)
    o = jnp.einsum("nbhgqk,nbkhd->nbqhgd", p.astype(v.dtype), vb, preferred_element_type=jnp.float32)
    o = o / jnp.transpose(denom, (0, 1, 4, 2, 3))[..., None]
    lse = jnp.transpose(m + jnp.log(denom), (0, 1, 4, 2, 3))
    o = o.reshape(n, lp, hkv, g, dh)[:, :L]
    lse = lse.reshape(n, lp, hkv, g)[:, :L]
    return o, lse


def dilated_mixture_attention(q, k, v):
    b, s, h, dh = q.shape
    outs, lses = [], []
    for window, dil in A_GROUPS:
        L = s // dil

        def to_classes(t):
            return t.reshape(b, L, dil, h, dh).transpose(0, 2, 1, 3, 4).reshape(b * dil, L, h, dh)

        o, lse = banded_attention(to_classes(q)[:, :, :, None], to_classes(k), to_classes(v), window // (2 * dil))
        o = o[:, :, :, 0].reshape(b, dil, L, h, dh).transpose(0, 2, 1, 3, 4).reshape(b, s, h, dh)
        lse = lse[..., 0].reshape(b, dil, L, h).transpose(0, 2, 1, 3).reshape(b, s, h)
        outs.append(o)
        lses.append(lse)
    wts = jax.nn.softmax(jnp.stack(lses, 0), axis=0)
    return jnp.einsum("gbsh,gbshd->bshd", wts, jnp.stack(outs, 0))


def encoder_layer(x, w_in, b_gate, sink, w_br_a, w_br_b, w_out, ln_g, ln_b):
    b, s, _ = x.shape
    h = jnp.einsum("bsd,de->bse", x, w_in)
    qa, ka, va, ga, qb, kb, vb, gb, pre_a, pre_b = jnp.split(h, IN_SPLIT_POINTS, axis=-1)
    pos = jnp.arange(s)
    qa = rope(qa.reshape(b, s, A_HEADS, HEAD_DIM), pos)
    ka = rope(ka.reshape(b, s, A_HEADS, HEAD_DIM), pos)
    va = va.reshape(b, s, A_HEADS, HEAD_DIM)
    ya = dilated_mixture_attention(qa, ka, va).reshape(b, s, A_WIDTH).astype(x.dtype) * jax.nn.silu(ga)
    qb = rope(qb.reshape(b, s, B_HEADS, HEAD_DIM), pos).reshape(b, s, B_KV_HEADS, B_GROUP, HEAD_DIM)
    kb = rope(kb.reshape(b, s, B_KV_HEADS, HEAD_DIM), pos)
    vb = vb.reshape(b, s, B_KV_HEADS, HEAD_DIM)
    ob, _ = banded_attention(qb, kb, vb, B_HALF_WINDOW, sink=sink.reshape(B_KV_HEADS, B_GROUP))
    yb = ob.reshape(b, s, B_WIDTH).astype(x.dtype) * jax.nn.silu(gb)
    gate_a = jax.nn.sigmoid(pre_a + b_gate[0])
    gate_b = jax.nn.sigmoid(pre_b + b_gate[1])
    merged = gate_a * jnp.einsum("bse,ed->bsd", ya, w_br_a) + gate_b * jnp.einsum("bse,ed->bsd", yb, w_br_b)
    out = jnp.einsum("bsd,de->bse", merged, w_out)
    return layer_norm(DEEPNORM_ALPHA * x + out, ln_g, ln_b)


def setup_inputs(seed: int = 0) -> dict:
    key = jax.random.key(seed)
    ks = jax.random.split(key, 10)
    f32 = jnp.float32
    x_prompt = jax.random.normal(ks[0], (BATCH, SEQ, D_MODEL), f32)
    x_sample = jax.random.normal(ks[1], (DEC_BATCH, DEC_SEQ, D_MODEL), f32)
    w_in = jax.random.normal(ks[2], (DEPTH, D_MODEL, IN_WIDTH), f32) * D_MODEL ** -0.5
    b_gate = jax.random.normal(ks[3], (DEPTH, 2, D_MODEL), f32) * 0.02
    sink_logit = jax.random.normal(ks[4], (DEPTH, B_HEADS), f32) * 0.5
    w_branch_a = jax.random.normal(ks[5], (DEPTH, A_WIDTH, D_MODEL), f32) * (A_WIDTH ** -0.5 * DEEPNORM_BETA)
    w_branch_b = jax.random.normal(ks[6], (DEPTH, B_WIDTH, D_MODEL), f32) * (B_WIDTH ** -0.5 * DEEPNORM_BETA)
    w_out = jax.random.normal(ks[7], (DEPTH, D_MODEL, D_MODEL), f32) * (D_MODEL ** -0.5 * DEEPNORM_BETA)
    ln_gain = 1.0 + 0.02 * jax.random.normal(ks[8], (DEPTH, D_MODEL), f32)
    ln_bias = 0.02 * jax.random.normal(ks[9], (DEPTH, D_MODEL), f32)
    return {"x_prompt": x_prompt, "x_sample": x_sample, "w_in": w_in, "b_gate": b_gate,
            "sink_logit": sink_logit, "w_branch_a": w_branch_a, "w_branch_b": w_branch_b,
            "w_out": w_out, "ln_gain": ln_gain, "ln_bias": ln_bias}


def reference(x_prompt, x_sample, w_in, b_gate, sink_logit, w_branch_a, w_branch_b, w_out, ln_gain, ln_bias):
    y_prompt = x_prompt
    y_sample = x_sample
    for l in range(DEPTH):
        y_prompt = encoder_layer(y_prompt, w_in[l], b_gate[l], sink_logit[l], w_branch_a[l], w_branch_b[l],
                                 w_out[l], ln_gain[l], ln_bias[l])
        y_sample = encoder_layer(y_sample, w_in[l], b_gate[l], sink_logit[l], w_branch_a[l], w_branch_b[l],
                                 w_out[l], ln_gain[l], ln_bias[l])
    return (y_prompt, y_sample)
```

```python
import math
from contextlib import ExitStack

import numpy as np
import concourse.bass as bass
import concourse.mybir as mybir
from concourse.bass_utils import run_bass_kernel_spmd

F32 = mybir.dt.float32
BF16 = mybir.dt.bfloat16
AF = mybir.ActivationFunctionType
ALU = mybir.AluOpType

D = 1024
HALO = 1024
NCORES = 8
LN_EPS = 1e-5
ALPHA = 2.0 ** 0.25
ROPE_THETA = 10000.0
GROUP_DILS = (1, 4, 16)

C_QA, C_KA, C_VA, C_GA = 0, 512, 1024, 1536
C_B = 2048
C_QB, C_KB, C_VB, C_GB = 2048, 2560, 2816, 2944
C_PRE = 3456
NW = C_PRE + 2048


class Sem:
    def __init__(self, nc, es, name):
        self.h = es.enter_context(nc.semaphore(name))
        self.n = 0


class Buf:
    __slots__ = ("lw", "rd", "dsem")

    def __init__(self, dsem=None):
        self.lw = None
        self.rd = []
        self.dsem = dsem


class Eng:
    def __init__(self, name, sem):
        self.name = name
        self.sem = sem
        self.prog = []
        self.waited = {}


class Prog:
    def __init__(self, nc, es):
        self.nc = nc
        self.es = es
        self.PE = Eng("pe", Sem(nc, es, "s_pe"))
        self.ACT = Eng("act", Sem(nc, es, "s_act"))
        self.DVE = Eng("dve", Sem(nc, es, "s_dve"))
        self.POOL = Eng("pool", Sem(nc, es, "s_pool"))
        self.SP = Eng("sp", Sem(nc, es, "s_sp"))
        self.nsem = 0
        self.dsems = []
        self.free_dsems = {"sw": [], "hw": []}
        self.kind = {}

    def newbuf(self, dma=False):
        if dma:
            kind = dma if isinstance(dma, str) else "hw"
            if self.free_dsems[kind]:
                return Buf(self.free_dsems[kind].pop())
            s = Sem(self.nc, self.es, "d%d" % self.nsem)
            self.nsem += 1
            self.dsems.append(s)
            self.kind[s] = kind
            return Buf(s)
        return Buf()

    def release(self, sems):
        for sm in sems:
            self.free_dsems[self.kind[sm]].append(sm)

    def _waits(self, E, reads, writes):
        waits = {}

        def need(t):
            if t is None:
                return
            sm, v = t
            if sm is E.sem and E.name == "pe":
                return
            if waits.get(sm, 0) < v:
                waits[sm] = v

        for b in reads:
            need(b.lw)
        for b in writes:
            need(b.lw)
            for t in b.rd:
                need(t)
        wl = [(sm, v) for sm, v in waits.items() if E.waited.get(sm, 0) < v]
        for sm, v in wl:
            E.waited[sm] = v
        return wl

    def op(self, E, fn, reads=(), writes=()):
        wl = self._waits(E, reads, writes)
        E.sem.n += 1
        tick = (E.sem, E.sem.n)
        E.prog.append((fn, wl, (E.sem, 1)))
        for b in reads:
            b.rd.append(tick)
        for b in writes:
            b.lw = tick
            b.rd = []
        return tick

    def dma(self, Q, out, in_, slot, reads=(), writes=()):
        wl = self._waits(Q, reads, writes)
        sm = slot.dsem
        assert self.kind[sm] == ("sw" if Q is self.POOL else "hw"), "semaphore shared between SW and HW DGE"
        sm.n += 16
        tick = (sm, sm.n)
        Q.prog.append((lambda e, o=out, i=in_: e.dma_start(out=o, in_=i), wl, (sm, 16)))
        for b in reads:
            b.rd.append(tick)
        for b in writes:
            b.lw = tick
            b.rd = []
        return tick

    def barrier(self, dsems=()):
        engs = [self.PE, self.ACT, self.DVE, self.POOL, self.SP]
        targets = [(e.sem, e.sem.n) for e in engs if e.sem.n > 0]
        targets += [(s, s.n) for s in dsems if s.n > 0]
        for E in engs:
            wl = [(sm, v) for sm, v in targets if sm is not E.sem and E.waited.get(sm, 0) < v]
            for sm, v in wl:
                E.waited[sm] = v
            if wl:
                E.prog.append((None, wl, None))

    def finish(self):
        nc = self.nc

        def run(E, e):
            for fn, wl, inc in E.prog:
                for sm, v in wl:
                    e.wait_ge(sm.h, v)
                if fn is None:
                    continue
                ins = fn(e)
                if inc is not None:
                    ins.then_inc(inc[0].h, inc[1])

        with nc.Block() as blk:
            @blk.tensor
            def _(e):
                run(self.PE, e)

            @blk.scalar
            def _(e):
                run(self.ACT, e)

            @blk.vector
            def _(e):
                run(self.DVE, e)

            @blk.gpsimd
            def _(e):
                run(self.POOL, e)

            @blk.sync
            def _(e):
                run(self.SP, e)


class Rot:
    def __init__(self, items):
        self.items = items
        self.i = 0

    def next(self):
        it = self.items[self.i % len(self.items)]
        self.i += 1
        return it


def build_program(seg_types, TQ, seg_halo=None):
    NSEG = len(seg_types)
    if seg_halo is None:
        seg_halo = [True] * NSEG
    NTYPE = max(seg_types) + 1
    TK = TQ + 2 * HALO
    NT = TK // 512
    NQT = TQ // 512
    QT0 = HALO // 512

    nc = bass.Bass("TRN2", target_bir_lowering=False)

    def din(name, shape, dt=F32):
        return nc.dram_tensor(name, list(shape), dt, kind="ExternalInput").ap()

    xT_d = din("xT", [NSEG, D, TK])
    xn_d = din("xn", [NSEG, TQ, D])
    w_d = din("w_in_p", [D, NW])
    wba_d = din("w_br_a", [512, D])
    wbb_d = din("w_br_b", [512, D])
    wo_d = din("w_out", [D, D])
    cs_d = din("cs", [NTYPE, 128, 2, TK])
    valid_d = din("valid", [NTYPE, TK, 64])
    masks_d = din("masks", [128, 1280])
    hb_d = din("bgate", [128, 16])
    sink_d = din("sinkrep", [128, 4])
    zrows_d = din("zrows", [HALO, 512])
    lng_d = din("lng", [128, D])
    lnb_d = din("lnb", [128, D])
    y_d = nc.dram_tensor("y", [NSEG, TQ, D], F32, kind="ExternalOutput").ap()

    va_d = nc.dram_tensor("va_scr", [NSEG, TK, 576], BF16, kind="Internal").ap()
    vb_d = nc.dram_tensor("vb_scr", [NSEG, TK, 192], BF16, kind="Internal").ap()
    yg_d = nc.dram_tensor("yg_scr", [NSEG, 2, 512, TQ], BF16, kind="Internal").ap()
    ya_d = nc.dram_tensor("ya_scr", [NSEG, 2, 512, TQ], BF16, kind="Internal").ap()

    dscr_d = nc.dram_tensor("dscr", [2, 2, TQ], F32, kind="Internal").ap()
    es = ExitStack()
    P = Prog(nc, es)
    PE, ACT, DVE, POOL, SP = P.PE, P.ACT, P.DVE, P.POOL, P.SP

    SZ_W = 8 * 2048 * 2
    SZ_WBO = (4 + 4 + 8) * 1024 * 2
    SZ_QT = 4 * TQ * 2
    SZ_KT = 4 * TK * 2
    SZ_CONST = 4096
    SZ_SH = 57344
    OFF_W = 0
    OFF_WBO = OFF_W + SZ_W
    OFF_CONST = OFF_WBO + SZ_WBO
    OFF_QT = OFF_CONST + SZ_CONST
    OFF_KT = OFF_QT + SZ_QT
    OFF_SH = OFF_KT + SZ_KT
    TOTAL = OFF_SH + SZ_SH
    arena = es.enter_context(nc.sbuf_tensor("arena", [128, TOTAL // 2], BF16))
    arena32 = arena.bitcast(F32)
    psum = es.enter_context(nc.psum_tensor("psum", [128, 8, 512], F32))

    def v16(off, n):
        assert off % 2 == 0
        return arena[:, off // 2: off // 2 + n]

    def v32(off, n):
        assert off % 4 == 0
        return arena32[:, off // 4: off // 4 + n]

    W = v16(OFF_W, 8 * 2048).rearrange("p (k n) -> p k n", k=8)
    WBA = v16(OFF_WBO, 4 * 1024).rearrange("p (k n) -> p k n", k=4)
    WBB = v16(OFF_WBO + 8192, 4 * 1024).rearrange("p (k n) -> p k n", k=4)
    WO = v16(OFF_WBO + 16384, 8 * 1024).rearrange("p (k n) -> p k n", k=8)
    QT = v16(OFF_QT, 4 * TQ).rearrange("p (c t) -> p c t", c=4)
    KT = v16(OFF_KT, 4 * TK).rearrange("p (c t) -> p c t", c=4)
    MASK = v16(OFF_CONST, 768)
    MBA = MASK[:, 0:256]
    MBB = MASK[:, 256:640]
    IDENT = MASK[:, 640:768]
    HB = v32(OFF_CONST + 2048, 16)
    ES = v32(OFF_CONST + 2048 + 64, 4)
    SINK = v32(OFF_CONST + 2048 + 64 + 16, 4)
    MHALF = v32(OFF_CONST + 2048 + 128, 1)
    ST8 = v32(OFF_CONST + 3072, 64)
    DROW = v32(OFF_CONST + 3072 + 256, 2 * (TQ // 128)).rearrange("p (h f) -> p h f", h=2)

    b_W = P.newbuf(dma="sw")
    b_WBO = P.newbuf(dma="sw")
    b_const = P.newbuf(dma="hw")
    b_mask = P.newbuf(dma="sw")
    b_QT = [[Buf() for _ in range(NQT)] for _ in range(4)]
    b_KT = [[Buf() for _ in range(NT)] for _ in range(4)]
    banks = [Buf() for _ in range(8)]

    b_va = [[Buf() for _ in range(NT)] for _ in range(NSEG)]
    b_vb = [[Buf() for _ in range(NT)] for _ in range(NSEG)]
    b_vva = [P.newbuf(dma="sw") for _ in range(NSEG)]
    b_vvb = [P.newbuf(dma="sw") for _ in range(NSEG)]
    b_yg = [[[Buf() for _ in range(4)] for _ in range(2)] for _ in range(NSEG)]
    b_ya = [[[Buf() for _ in range(4)] for _ in range(2)] for _ in range(NSEG)]

    wv = w_d.rearrange("(k p) n -> p k n", p=128)

    def load_W(c0, ncols):
        for k in range(8):
            P.dma(POOL, W[:, k, 0:ncols], wv[:, k, c0:c0 + ncols], b_W, writes=[b_W])

    load_W(0, 2048)
    P.dma(POOL, MASK, masks_d[:, 0:768], b_mask, writes=[b_mask])
    P.dma(SP, HB, hb_d, b_const, writes=[b_const])
    P.dma(SP, SINK, sink_d, b_const, writes=[b_const])
    P.dma(POOL, WBA, wba_d.rearrange("(k p) n -> p k n", p=128), b_WBO, writes=[b_WBO])
    P.dma(POOL, WBB, wbb_d.rearrange("(k p) n -> p k n", p=128), b_WBO, writes=[b_WBO])
    for k in range(8):
        P.dma(POOL, WO[:, k, :], wo_d[k * 128:(k + 1) * 128, :], b_WBO, writes=[b_WBO])
    for s in range(NSEG):
        P.dma(POOL, va_d[s, :, 512:576], valid_d[seg_types[s]], b_vva[s], writes=[b_vva[s]])
        P.dma(POOL, vb_d[s, :, 128:192], valid_d[seg_types[s]], b_vvb[s], writes=[b_vvb[s]])
    halo_cts = [ct for ct in range(NT) if not (QT0 <= ct < QT0 + NQT)]
    for s in range(NSEG):
        if seg_halo[s]:
            continue
        for (lo, cts) in ((0, [ct for ct in halo_cts if ct < QT0]), (HALO + TQ, [ct for ct in halo_cts if ct >= QT0])):
            P.dma(POOL, va_d[s, lo:lo + HALO, 0:512], zrows_d, b_vva[s], writes=[b_va[s][ct] for ct in cts])
            P.dma(POOL, vb_d[s, lo:lo + HALO, 0:128], zrows_d[:, 0:128], b_vvb[s], writes=[b_vb[s][ct] for ct in cts])
    P.op(ACT, lambda e: e.activation(out=ES, in_=SINK, func=AF.Exp), reads=[b_const], writes=[b_const])
    P.op(DVE, lambda e: e.memset(MHALF, -0.5), writes=[b_const])
    P.op(DVE, lambda e: e.tensor_scalar(out=HB, in0=HB, scalar1=0.5, scalar2=None, op0=ALU.mult), reads=[b_const], writes=[b_const])

    def phase_proj(s, mixer):
        typ = seg_types[s]
        o = OFF_SH
        XS = [v16(o + i * 8192, 8 * 512).rearrange("p (k t) -> p k t", k=8) for i in range(2)]
        o += 16384
        TB = [v32(o + i * 4096, 1024).rearrange("p (a t) -> p a t", a=2) for i in range(2)]
        o += 8192
        TMPA = [v32(o + i * 2048, 512) for i in range(2)]
        o += 4096
        TMPB = [v32(o + i * 2048, 512) for i in range(2)]
        o += 4096
        STG = [v16(o + i * 1024, 512) for i in range(8)]
        o += 8192
        XF = [v32(o + i * 8192, 4 * 512).rearrange("p (k t) -> p k t", k=4) for i in range(2)]
        o += 16384
        assert o <= OFF_SH + SZ_SH
        xs_b = Rot([(XS[i], Buf()) for i in range(2)])
        xf_b = [(XF[i], P.newbuf(dma=True)) for i in range(2)]
        tb_b = Rot([(TB[i], P.newbuf(dma=True)) for i in range(2)])
        ta_b = Rot([(TMPA[i], Buf()) for i in range(2)])
        tbb_b = Rot([(TMPB[i], Buf()) for i in range(2)])
        stg_b = Rot([(STG[i], P.newbuf(dma=True)) for i in range(8)])
        bk = Rot([(psum[:, i, :], banks[i]) for i in range(8)])
        local_dsems = [b.dsem for _, b in xf_b + tb_b.items + stg_b.items]

        if mixer == 0:
            cq, ck, cv, cg = 0, 512, 1024, 1536
            nk, vw = 4, 512
            v_scr, bv = va_d, b_va
        else:
            cq, ck, cv, cg = 0, 512, 768, 896
            nk, vw = 2, 128
            v_scr, bv = vb_d, b_vb

        xv = xT_d[s].rearrange("(k p) t -> p k t", p=128)
        shuf = [i ^ 1 for i in range(32)]

        def xload(ct):
            for hh in range(2):
                xf, xfb = xf_b[hh]
                P.dma(SP, xf, xv[:, 4 * hh:4 * hh + 4, ct * 512:(ct + 1) * 512], xfb, writes=[xfb])

        tbl = {}

        def tload(ct):
            tb, tbuf = tb_b.next()
            P.dma(SP, tb, cs_d[typ, :, :, ct * 512:(ct + 1) * 512], tbuf, writes=[tbuf])
            tbl[ct] = (tb, tbuf)

        def do_cast():
            xs, xb = xs_b.next()
            for hh in range(2):
                xf, xfb = xf_b[hh]
                P.op(ACT, lambda e, xs=xs, xf=xf, hh=hh: e.activation(out=xs[:, 4 * hh:4 * hh + 4, :], in_=xf, func=AF.Copy), reads=[xfb], writes=[xb])
            return xs, xb

        if seg_halo[s]:
            cts = list(range(NT))
        else:
            cts = [ct for ct in range(NT) if QT0 <= ct < QT0 + NQT]
            for (lo, hc) in ((0, [ct for ct in halo_cts if ct < QT0]), (HALO + TQ, [ct for ct in halo_cts if ct >= QT0])):
                P.op(POOL, lambda e, lo=lo: e.memset(KT[:, 0:nk, lo:lo + HALO], 0.0), writes=[b_KT[c][ct] for c in range(nk) for ct in hc])
        xload(cts[0])
        tload(cts[0])
        nxt = do_cast()
        for ci_, ct in enumerate(cts):
            c0 = ct * 512
            isq = QT0 <= ct < QT0 + NQT
            xs, xb = nxt
            tb, tbuf = tbl.pop(ct)
            if ci_ + 1 < len(cts):
                xload(cts[ci_ + 1])
                tload(cts[ci_ + 1])

            def proj(col, xs=xs):
                ps, pb = bk.next()

                def f(e, ps=ps, col=col, xs=xs):
                    for k in range(8):
                        ins = e.matmul(ps, lhsT=W[:, k, col:col + 128], rhs=xs[:, k, :], start=(k == 0), stop=(k == 7))
                    return ins
                P.op(PE, f, reads=[b_W, xb], writes=[pb])
                return ps, pb

            def rope(ps, pb, dst, dbuf, tb=tb, tbuf=tbuf):
                ta, tab = ta_b.next()
                t2, t2b = tbb_b.next()
                P.op(DVE, lambda e: e.stream_shuffle(out=ta, in_=ps, mask=shuf), reads=[pb], writes=[tab])
                P.op(DVE, lambda e: e.tensor_tensor(out=t2, in0=ps, in1=tb[:, 0, :], op=ALU.mult), reads=[pb, tbuf], writes=[t2b])
                P.op(DVE, lambda e: e.tensor_tensor(out=ta, in0=ta, in1=tb[:, 1, :], op=ALU.mult), reads=[tab, tbuf], writes=[tab])
                P.op(DVE, lambda e: e.tensor_tensor(out=dst, in0=t2, in1=ta, op=ALU.add), reads=[tab, t2b], writes=[dbuf])

            for c in range(nk):
                ps, pb = proj(ck + c * 128)
                rope(ps, pb, KT[:, c, c0:c0 + 512], b_KT[c][ct])
            if isq:
                qi = ct - QT0
                q0 = qi * 512
                for c in range(4):
                    ps, pb = proj(cq + c * 128)
                    rope(ps, pb, QT[:, c, q0:q0 + 512], b_QT[c][qi])
                for c in range(4):
                    ps, pb = proj(cg + c * 128)
                    ta, tab = ta_b.next()
                    sg, sgb = stg_b.next()
                    P.op(ACT, lambda e, ta=ta, ps=ps: e.activation(out=ta, in_=ps, func=AF.Tanh, scale=0.5), reads=[pb], writes=[tab])
                    P.op(DVE, lambda e, ta=ta, ps=ps, sg=sg: e.scalar_tensor_tensor(out=sg, in0=ta, scalar=1.0, in1=ps, op0=ALU.add, op1=ALU.mult),
                         reads=[tab, pb], writes=[sgb])
                    P.dma(SP, yg_d[s, mixer, c * 128:(c + 1) * 128, q0:q0 + 512], sg, sgb, reads=[sgb], writes=[b_yg[s][mixer][c]])
            if ci_ + 1 < len(cts):
                nxt = do_cast()
            for tt in range(4):
                ps, pb = bk.next()

                def f(e, ps=ps, xs=xs, tt=tt):
                    for k in range(8):
                        ins = e.matmul(ps[:, 0:vw], lhsT=xs[:, k, tt * 128:(tt + 1) * 128], rhs=W[:, k, cv:cv + vw], start=(k == 0), stop=(k == 7))
                    return ins
                P.op(PE, f, reads=[b_W, xb], writes=[pb])
                sg, sgb = stg_b.next()
                P.op(ACT, lambda e, sg=sg, ps=ps: e.activation(out=sg[:, 0:vw], in_=ps[:, 0:vw], func=AF.Copy), reads=[pb], writes=[sgb])
                r0 = c0 + tt * 128
                P.dma(SP, v_scr[s, r0:r0 + 128, 0:vw], sg[:, 0:vw], sgb, reads=[sgb], writes=[bv[s][ct]])
        if mixer == 0:
            load_W(C_B, 1408)
        else:
            load_W(C_PRE, 2048)
        P.barrier(local_dsems)
        P.release(local_dsems)

    def phase_attn(s, mixer):
        o = OFF_SH
        ACCO = v32(o, TQ)
        o += TQ * 4
        ACCD = v32(o, TQ)
        o += TQ * 4
        SGP = v16(o, TQ)
        o += TQ * 2
        YP = SGP
        PT = [v16(o + i * 1536, 768).rearrange("p (h q) -> p h q", h=2) for i in range(3)]
        o += 3 * 1536
        vwid = 576 if mixer == 0 else 192
        NVT = 6
        VT = [v16(o + i * 1152, vwid) for i in range(NVT)]
        o += NVT * 1152
        MULM = v16(o, 512).rearrange("p (r h q) -> p r h q", r=2, h=2)
        o += 4096
        assert o <= OFF_SH + SZ_SH, (o, OFF_SH + SZ_SH)
        ACH = 1024
        b_accs = [Buf() for _ in range(TQ // ACH)]
        b_rscr = Buf()
        b_fin = P.newbuf(dma="sw")
        b_mulm = P.newbuf(dma="sw")
        if mixer == 1:
            P.dma(POOL, MULM, masks_d[:, 768:1280], b_mulm, writes=[b_mulm])
        b_drows = [Buf(), Buf()]
        b_dscr = [Buf(), Buf()]
        pending = []

        def flush():
            while pending:
                pending.pop(0)()
        b_sgp = P.newbuf(dma=True)
        b_yp = b_sgp
        pt_b = Rot([(PT[i], Buf()) for i in range(3)])
        vt_b = Rot([(VT[i], P.newbuf(dma=True)) for i in range(NVT)])
        local_dsems = [b_sgp.dsem, b_fin.dsem, b_mulm.dsem] + [b.dsem for _, b in vt_b.items]
        scale = 1.0 / 8.0
        st_cnt = [0]
        bf_cnt = [0]

        if mixer == 0:
            v_scr, bv, bvv = va_d, b_va, b_vva[s]
            chains = []
            for d in GROUP_DILS:
                for r in range(d):
                    chains.append((d, r))
            roles = ((1, 1), (0, 0))
            ones_c0 = 512
        else:
            v_scr, bv, bvv = vb_d, b_vb, b_vvb[s]
            chains = [(1, 0)]
            roles = ((2, 1), (1, None), (0, 0))
            ones_c0 = 128
        maxdb = max(r[0] for r in roles)

        for pair in range(4):
            kc = pair if mixer == 0 else pair // 2
            first_chain = True
            for (d, r) in chains:
                NB = TQ // (128 * d)
                if mixer == 0:
                    kbase = r + HALO - 64 * d
                else:
                    kbase = HALO - 128
                for B0 in range(0, NB, 4):
                    nb = min(4, NB - B0)
                    j = bf_cnt[0] % 2
                    bf_cnt[0] += 1
                    OB, ob = psum[:, 2 + 2 * j, :], banks[2 + 2 * j]
                    DB, db_ = psum[:, 3 + 2 * j, :], banks[3 + 2 * j]
                    started = [False, False]
                    for t in range(B0, B0 + nb + maxdb):
                        served = []
                        for (dbk, role) in roles:
                            b = t - dbk
                            if B0 <= b < B0 + nb:
                                served.append((b - B0, role))
                        if not served:
                            continue
                        n = 128 * len(served)
                        rel0 = kbase + 128 * d * t
                        vt, vtb = vt_b.next()
                        tiles = set(range(rel0 // 512, (rel0 + 127 * d) // 512 + 1))
                        P.dma(SP, vt, v_scr[s, rel0:rel0 + 127 * d + 1:d, :], vtb,
                              reads=[bv[s][x] for x in tiles] + [bvv], writes=[vtb])
                        half = st_cnt[0] % 2
                        st_cnt[0] += 1
                        pt, ptb = pt_b.next()
                        kreads = [b_KT[kc][x] for x in tiles]
                        qreads = []
                        for (cb, role) in served:
                            q0 = r + 128 * d * (B0 + cb)
                            for x in range(q0 // 512, (q0 + 127 * d) // 512 + 1):
                                if b_QT[pair][x] not in qreads:
                                    qreads.append(b_QT[pair][x])

                        sb0 = 0 if half == 0 else 6

                        if mixer == 0:
                            mcol = {1: 0, 0: 128}
                            MB = MBA
                        else:
                            mcol = {1: 0, None: 128, 0: 256}
                            MB = MBB
                        m0 = mcol[served[0][1]]
                        need_mask = (mixer == 0) and any(role is not None for _, role in served)

                        def fqk(e, served=served, rel0=rel0, sb0=sb0, d=d, r=r, B0=B0, kc=kc, pair=pair, n=n, m0=m0, MB=MB, need_mask=need_mask):
                            ins = None
                            q0 = r + 128 * d * (B0 + served[0][0])
                            for h in range(2):
                                ins = e.matmul(psum[:, sb0 + h, 0:n],
                                               lhsT=KT[64 * h:64 * h + 64, kc, rel0:rel0 + 127 * d + 1:d],
                                               rhs=QT[64 * h:64 * h + 64, pair, q0:q0 + (n - 1) * d + 1:d],
                                               start=True, stop=not need_mask)
                            if need_mask:
                                for h in range(2):
                                    ins = e.matmul(psum[:, sb0 + h, 0:n], lhsT=IDENT, rhs=MB[:, m0:m0 + n], start=False, stop=True,
                                                   skip_group_check=True)
                            return ins
                        P.op(PE, fqk, reads=kreads + qreads + [b_mask], writes=[banks[sb0], banks[sb0 + 1]])

                        P.op(ACT, lambda e, pt=pt, n=n, sb0=sb0: e.activation(out=pt[:, :, 0:n], in_=psum[:, sb0:sb0 + 2, 0:n], func=AF.Exp, scale=scale),
                             reads=[banks[sb0], banks[sb0 + 1]], writes=[ptb])

                        if mixer == 1:
                            for i, (cb, role) in enumerate(served):
                                if role is None:
                                    continue
                                P.op(DVE, lambda e, pt=pt, i=i, role=role: e.tensor_tensor(out=pt[:, :, i * 128:(i + 1) * 128], in0=pt[:, :, i * 128:(i + 1) * 128],
                                                                                          in1=MULM[:, role, :, :], op=ALU.mult),
                                     reads=[b_mulm], writes=[ptb])
                        st_flags = [not started[0], not started[1]]
                        started[0] = started[1] = True

                        def fpv(e, served=served, pt=pt, vt=vt, OB=OB, DB=DB, pair=pair, st_flags=st_flags, n=n):
                            ins = None
                            c0 = served[0][0] * 128
                            for h in range(2):
                                if mixer == 0:
                                    vcol = (2 * pair + h) * 64
                                else:
                                    vcol = (pair // 2) * 64
                                e.matmul(OB[64 * h:64 * h + 64, c0:c0 + n], lhsT=vt[:, vcol:vcol + 64],
                                         rhs=pt[:, h, 0:n], start=st_flags[h], stop=False, skip_group_check=True)
                                ins = e.matmul(DB[64 * h:64 * h + 64, c0:c0 + n], lhsT=vt[:, ones_c0:ones_c0 + 64],
                                               rhs=pt[:, h, 0:n], start=st_flags[h], stop=False, skip_group_check=True)
                            return ins
                        flush()
                        pending.append(lambda fpv=fpv, ptb=ptb, vtb=vtb, ob=ob, db_=db_: P.op(PE, fpv, reads=[ptb, vtb], writes=[ob, db_]))
                    qs = r + 128 * d * B0
                    qe = qs + (128 * nb - 1) * d + 1
                    nn = 128 * nb
                    accb = [b_accs[x] for x in range(qs // ACH, (qe - 1) // ACH + 1)]

                    def evac(qs=qs, qe=qe, d=d, nn=nn, OB=OB, DB=DB, ob=ob, db_=db_, accb=accb, fc=first_chain, pair=pair):
                        if fc:
                            P.op(ACT, lambda e: e.activation(out=ACCO[:, qs:qe:d], in_=OB[:, 0:nn], func=AF.Copy), reads=[ob], writes=accb)
                            if mixer == 1:
                                P.op(DVE, lambda e: e.tensor_scalar(out=ACCD[:, qs:qe:d], in0=DB[:, 0:nn], scalar1=ES[:, pair:pair + 1], scalar2=None, op0=ALU.add),
                                     reads=[db_, b_const], writes=accb)
                            else:
                                P.op(DVE, lambda e: e.tensor_copy(out=ACCD[:, qs:qe:d], in_=DB[:, 0:nn]), reads=[db_], writes=accb)
                        else:
                            P.op(DVE, lambda e: e.tensor_tensor(out=ACCO[:, qs:qe:d], in0=OB[:, 0:nn], in1=ACCO[:, qs:qe:d], op=ALU.add),
                                 reads=[ob], writes=accb)
                            P.op(DVE, lambda e: e.tensor_tensor(out=ACCD[:, qs:qe:d], in0=DB[:, 0:nn], in1=ACCD[:, qs:qe:d], op=ALU.add),
                                 reads=[db_], writes=accb)
                    pending.append(evac)
                first_chain = False
            flush()
            P.dma(SP, SGP, yg_d[s, mixer, pair * 128:(pair + 1) * 128, :], b_sgp, reads=[b_yg[s][mixer][pair]], writes=[b_sgp])
            for h in range(2):
                P.dma(POOL, DROW[:, h, :], ACCD[64 * h:64 * h + 1, :], b_fin, reads=b_accs, writes=[b_drows[h]])
            P.op(DVE, lambda e: e.reciprocal(out=DROW, in_=DROW), reads=[], writes=b_drows)
            P.dma(POOL, dscr_d[1].rearrange("h (p f) -> p h f", p=128), DROW, b_fin, reads=b_drows, writes=[b_dscr[1]])
            for h in range(2):
                P.dma(POOL, ACCD[64 * h:64 * h + 64, :], dscr_d[1, h, :].partition_broadcast(64), b_fin, reads=[b_dscr[1]], writes=b_accs)
            CH = ACH
            for ci, c0 in enumerate(range(0, TQ, CH)):
                ab = [b_accs[ci]]
                P.op(DVE, lambda e, c0=c0: e.scalar_tensor_tensor(out=ACCO[:, c0:c0 + CH], in0=ACCO[:, c0:c0 + CH], scalar=0.5, in1=ACCD[:, c0:c0 + CH],
                                                                 op0=ALU.mult, op1=ALU.mult), reads=[], writes=ab)
                P.op(DVE, lambda e, c0=c0: e.tensor_tensor(out=YP[:, c0:c0 + CH], in0=ACCO[:, c0:c0 + CH], in1=SGP[:, c0:c0 + CH], op=ALU.mult),
                     reads=ab, writes=[b_sgp])
            P.dma(POOL, ya_d[s, mixer, pair * 128:(pair + 1) * 128, :], YP, b_fin, reads=[b_yp], writes=[b_ya[s][mixer][pair]])
        P.barrier(local_dsems)
        P.release(local_dsems)

    def phase_out(s):
        o = OFF_QT
        XS = [v16(o + i * 8192, 8 * 512).rearrange("p (k t) -> p k t", k=8) for i in range(2)]
        o += 16384
        YAB = [v16(o + i * 8192, 8 * 512).rearrange("p (k t) -> p k t", k=8) for i in range(2)]
        o += 16384
        G = v16(o, 16 * 512).rearrange("p (c t) -> p c t", c=16)
        o += 16384
        MG = v16(o, 8 * 512).rearrange("p (c t) -> p c t", c=8)
        o += 8192
        TMP = [v32(o + i * 2048, 512) for i in range(2)]
        o += 4096
        XR = [v32(o + i * 4096, 1024) for i in range(2)]
        o += 8192
        ZZ = [v32(o + i * 4096, 1024) for i in range(2)]
        o += 8192
        LNG = v32(o, 1024)
        o += 4096
        LNB = v32(o, 1024)
        o += 4096
        XF = [v32(o + i * 8192, 4 * 512).rearrange("p (k t) -> p k t", k=4) for i in range(2)]
        o += 16384
        assert o <= TOTAL
        xs_b = Rot([(XS[i], Buf()) for i in range(2)])
        xf_b = [(XF[i], P.newbuf(dma=True)) for i in range(2)]
        yab_b = Rot([(YAB[i], P.newbuf(dma=True)) for i in range(2)])
        tmp_b = Rot([(TMP[i], Buf()) for i in range(2)])
        xr_b = Rot([(XR[i], P.newbuf(dma=True)) for i in range(2)])
        zz_b = Rot([(ZZ[i], P.newbuf(dma="sw")) for i in range(2)])
        b_G = [Buf() for _ in range(16)]
        b_MG = [Buf() for _ in range(8)]
        b_ln = P.newbuf(dma=True)
        b_sts = [Buf(), Buf()]
        bk = Rot([(psum[:, i, :], banks[i]) for i in range(8)])
        local_dsems = [b.dsem for _, b in xf_b + yab_b.items + xr_b.items + zz_b.items] + [b_ln.dsem]

        P.dma(SP, LNG, lng_d, b_ln, writes=[b_ln])
        P.dma(SP, LNB, lnb_d, b_ln, writes=[b_ln])
        xv = xT_d[s].rearrange("(k p) t -> p k t", p=128)
        def xload(qi):
            c0 = HALO + qi * 512
            for hh in range(2):
                xf, xfb = xf_b[hh]
                P.dma(SP, xf, xv[:, 4 * hh:4 * hh + 4, c0:c0 + 512], xfb, writes=[xfb])

        ytl = {}

        def yload(qi):
            yab, yb = yab_b.next()
            for m in range(2):
                P.dma(SP, yab[:, 4 * m:4 * m + 4, :], ya_d[s, m].rearrange("(c p) t -> p c t", p=128)[:, :, qi * 512:(qi + 1) * 512], yb,
                      reads=b_ya[s][m], writes=[yb])
            ytl[qi] = (yab, yb)

        def do_cast():
            xs, xb = xs_b.next()
            for hh in range(2):
                xf, xfb = xf_b[hh]
                P.op(ACT, lambda e, xs=xs, xf=xf, hh=hh: e.activation(out=xs[:, 4 * hh:4 * hh + 4, :], in_=xf, func=AF.Copy), reads=[xfb], writes=[xb])
            return xs, xb

        xload(0)
        yload(0)
        nxt = do_cast()
        for qi in range(NQT):
            q0 = qi * 512
            xs, xb = nxt
            yab, yb = ytl.pop(qi)
            if qi + 1 < NQT:
                xload(qi + 1)
                yload(qi + 1)
            for cg in range(16):
                ps, pb = bk.next()

                def f(e, ps=ps, cg=cg, xs=xs):
                    for k in range(8):
                        ins = e.matmul(ps, lhsT=W[:, k, cg * 128:(cg + 1) * 128], rhs=xs[:, k, :], start=(k == 0), stop=(k == 7))
                    return ins
                P.op(PE, f, reads=[b_W, xb], writes=[pb])
                P.op(ACT, lambda e, ps=ps, cg=cg: e.activation(out=G[:, cg, :], in_=ps, func=AF.Tanh, scale=0.5, bias=HB[:, cg:cg + 1]),
                     reads=[pb, b_const], writes=[b_G[cg]])
            if qi + 1 < NQT:
                nxt = do_cast()
            for dc in range(8):
                psa, pba = bk.next()
                psb, pbb = bk.next()

                def fa(e, psa=psa, dc=dc, yab=yab):
                    for k in range(4):
                        ins = e.matmul(psa, lhsT=WBA[:, k, dc * 128:(dc + 1) * 128], rhs=yab[:, k, :], start=(k == 0), stop=(k == 3))
                    return ins

                def fb(e, psb=psb, dc=dc, yab=yab):
                    for k in range(4):
                        ins = e.matmul(psb, lhsT=WBB[:, k, dc * 128:(dc + 1) * 128], rhs=yab[:, 4 + k, :], start=(k == 0), stop=(k == 3))
                    return ins
                P.op(PE, fa, reads=[b_WBO, yb], writes=[pba])
                P.op(PE, fb, reads=[b_WBO, yb], writes=[pbb])
                t1, t1b = tmp_b.next()
                t2, t2b = tmp_b.next()
                P.op(DVE, lambda e, t1=t1, psa=psa, dc=dc: e.scalar_tensor_tensor(out=t1, in0=G[:, dc, :], scalar=1.0, in1=psa, op0=ALU.add, op1=ALU.mult),
                     reads=[pba, b_G[dc]], writes=[t1b])
                P.op(DVE, lambda e, t2=t2, psb=psb, dc=dc: e.scalar_tensor_tensor(out=t2, in0=G[:, 8 + dc, :], scalar=1.0, in1=psb, op0=ALU.add, op1=ALU.mult),
                     reads=[pbb, b_G[8 + dc]], writes=[t2b])
                P.op(DVE, lambda e, t1=t1, t2=t2, dc=dc: e.tensor_tensor(out=MG[:, dc, :], in0=t1, in1=t2, op=ALU.add),
                     reads=[t1b, t2b], writes=[b_MG[dc]])
            for tt in range(4):
                r0 = q0 + tt * 128
                sb = 32 * (tt % 2)
                bst = b_sts[tt % 2]
                xr, xrb = xr_b.next()
                P.dma(SP, xr, xn_d[s, r0:r0 + 128, :], xrb, writes=[xrb])
                zz, zzb = zz_b.next()
                P.op(ACT, lambda e, xr=xr: e.activation(out=xr, in_=xr, func=AF.Copy, scale=ALPHA), reads=[xrb], writes=[xrb])
                for hf in range(2):
                    ps, pb = bk.next()

                    def f(e, ps=ps, tt=tt, hf=hf):
                        for k in range(8):
                            ins = e.matmul(ps, lhsT=MG[:, k, tt * 128:(tt + 1) * 128], rhs=WO[:, k, hf * 512:(hf + 1) * 512], start=(k == 0), stop=(k == 7))
                        return ins
                    P.op(PE, f, reads=[b_WBO] + b_MG, writes=[pb])
                    P.op(DVE, lambda e, zz=zz, ps=ps, xr=xr, hf=hf: e.scalar_tensor_tensor(out=zz[:, hf * 512:(hf + 1) * 512], in0=ps, scalar=0.5,
                                                                                          in1=xr[:, hf * 512:(hf + 1) * 512], op0=ALU.mult, op1=ALU.add),
                         reads=[pb, xrb], writes=[zzb])
                    P.op(DVE, lambda e, zz=zz, hf=hf, sb=sb: e.bn_stats(out=ST8[:, sb + hf * 6:sb + (hf + 1) * 6], in_=zz[:, hf * 512:(hf + 1) * 512]),
                         reads=[zzb], writes=[bst])
                P.op(DVE, lambda e, sb=sb: e.bn_aggr(out=ST8[:, sb + 12:sb + 14], in_=ST8[:, sb + 0:sb + 12]), reads=[bst], writes=[bst])
                P.op(DVE, lambda e, sb=sb: e.tensor_scalar(out=ST8[:, sb + 14:sb + 15], in0=ST8[:, sb + 13:sb + 14], scalar1=LN_EPS, scalar2=None, op0=ALU.add), reads=[bst], writes=[bst])
                P.op(POOL, lambda e, sb=sb: e.tensor_tensor(out=ST8[:, sb + 15:sb + 16], in0=ST8[:, sb + 14:sb + 15], in1=MHALF, op=ALU.pow), reads=[bst, b_const], writes=[bst])
                P.op(DVE, lambda e, sb=sb: e.tensor_scalar(out=ST8[:, sb + 16:sb + 17], in0=ST8[:, sb + 12:sb + 13], scalar1=ST8[:, sb + 15:sb + 16], scalar2=-1.0, op0=ALU.mult, op1=ALU.mult),
                     reads=[bst], writes=[bst])
                P.op(ACT, lambda e, zz=zz, sb=sb: e.activation(out=zz, in_=zz, func=AF.Identity, scale=ST8[:, sb + 15:sb + 16], bias=ST8[:, sb + 16:sb + 17]),
                     reads=[zzb, bst], writes=[zzb])
                P.op(DVE, lambda e, zz=zz: e.tensor_tensor(out=zz, in0=zz, in1=LNG, op=ALU.mult), reads=[zzb, b_ln], writes=[zzb])
                P.op(POOL, lambda e, zz=zz: e.tensor_tensor(out=zz, in0=zz, in1=LNB, op=ALU.add), reads=[zzb, b_ln], writes=[zzb])
                P.dma(POOL, y_d[s, r0:r0 + 128, :], zz, zzb, reads=[zzb])
        if s + 1 < NSEG:
            load_W(0, 2048)
        P.barrier(local_dsems)
        P.release(local_dsems)

    for s in range(NSEG):
        phase_proj(s, 0)
        phase_attn(s, 0)
        phase_proj(s, 1)
        phase_attn(s, 1)
        phase_out(s)
    P.barrier(P.dsems)
    P.finish()
    es.close()
    return nc


def _perm_cols_interleave(w, nheads):
    idx = np.empty(64, np.int64)
    idx[0::2] = np.arange(32)
    idx[1::2] = 32 + np.arange(32)
    cols = np.concatenate([h * 64 + idx for h in range(nheads)])
    return w[:, cols]


def make_w_in_p(w_in):
    qa, ka, va, ga = w_in[:, 0:512], w_in[:, 512:1024], w_in[:, 1024:1536], w_in[:, 1536:2048]
    qb, kb, vb, gb = w_in[:, 2048:2560], w_in[:, 2560:2688], w_in[:, 2688:2816], w_in[:, 2816:3328]
    pre = w_in[:, 3328:5376]
    kbp = _perm_cols_interleave(kb, 2)
    kbdup = np.concatenate([kbp[:, 0:64], kbp[:, 0:64], kbp[:, 64:128], kbp[:, 64:128]], axis=1)
    out = np.concatenate([_perm_cols_interleave(qa, 8), _perm_cols_interleave(ka, 8), va, ga,
                          _perm_cols_interleave(qb, 8), kbdup, vb, gb, pre], axis=1)
    assert out.shape[1] == NW
    return np.ascontiguousarray(out, dtype=np.float32)


def make_tables(start, seq_len, TK):
    pos = np.arange(TK, dtype=np.int64) - HALO + start
    half = 32
    inv_freq = (np.float32(ROPE_THETA) ** (-np.arange(half, dtype=np.float32) / np.float32(half))).astype(np.float32)
    ang = pos.astype(np.float32)[None, :] * inv_freq[:, None]
    cos = np.cos(ang.astype(np.float64)).astype(np.float32)
    sin = np.sin(ang.astype(np.float64)).astype(np.float32)
    rows = np.arange(128)
    fi = (rows % 64) // 2
    sign = np.where(rows % 2 == 0, -1.0, 1.0).astype(np.float32)
    cs = np.empty((128, 2, TK), np.float32)
    cs[:, 0, :] = cos[fi]
    cs[:, 1, :] = sin[fi] * sign[:, None]
    valid = ((pos >= 0) & (pos < seq_len)).astype(np.float32)
    valid = np.repeat(valid[:, None], 64, axis=1)
    return cs, np.ascontiguousarray(valid)


def make_consts(b_gate, sink_logit, ln_gain, ln_bias):
    k = np.arange(128)[:, None]
    q = np.arange(128)[None, :]
    mU = (k >= q).astype(np.float32)
    mL = (k <= q).astype(np.float32)
    NEG = np.float32(-30000.0)
    bU = np.where(mU > 0, np.float32(0), NEG).astype(np.float32)
    bL = np.where(mL > 0, np.float32(0), NEG).astype(np.float32)
    masks = np.concatenate([bL, bU, bL, np.zeros((128, 128), np.float32), bU, np.eye(128, dtype=np.float32)], axis=1)
    masks = np.ascontiguousarray(np.concatenate([masks, mU, mU, mL, mL], axis=1))
    hb = np.ascontiguousarray(b_gate.reshape(16, 128).T.astype(np.float32))
    sinkrep = np.empty((128, 4), np.float32)
    for p in range(4):
        sinkrep[0:64, p] = sink_logit[2 * p]
        sinkrep[64:128, p] = sink_logit[2 * p + 1]
    lng = np.ascontiguousarray(np.broadcast_to(ln_gain[None, :], (128, D)), dtype=np.float32)
    lnb = np.ascontiguousarray(np.broadcast_to(ln_bias[None, :], (128, D)), dtype=np.float32)
    return masks, hb, sinkrep, lng, lnb


def seg_inputs(x_seq, start, TQ):
    S = x_seq.shape[0]
    TK = TQ + 2 * HALO
    xT = np.zeros((D, TK), np.float32)
    lo = max(0, start - HALO)
    hi = min(S, start + TQ + HALO)
    xT[:, lo - (start - HALO):hi - (start - HALO)] = x_seq[lo:hi].T
    return xT, np.ascontiguousarray(x_seq[start:start + TQ])


_NC_CACHE = {}


def kernel(x_prompt, x_sample, w_in, b_gate, sink_logit, w_branch_a, w_branch_b, w_out, ln_gain, ln_bias):
    x_prompt = np.asarray(x_prompt, np.float32)
    x_sample = np.asarray(x_sample, np.float32)
    TQ = 4096
    TK = TQ + 2 * HALO
    w_in_p = make_w_in_p(np.asarray(w_in, np.float32)[0])
    masks, hb, sinkrep, lng, lnb = make_consts(np.asarray(b_gate, np.float32)[0], np.asarray(sink_logit, np.float32)[0],
                                               np.asarray(ln_gain, np.float32)[0], np.asarray(ln_bias, np.float32)[0])
    cs_s, valid_s = make_tables(0, 4096, TK)
    key = ("full",)
    if key not in _NC_CACHE:
        _NC_CACHE[key] = build_program([0, 0, 1], TQ, seg_halo=[False, False, True])
    nc = _NC_CACHE[key]
    in_maps = []
    for c in range(NCORES):
        segs = [seg_inputs(x_sample[2 * c], 0, TQ), seg_inputs(x_sample[2 * c + 1], 0, TQ),
                seg_inputs(x_prompt[c // 2], (c % 2) * TQ, TQ)]
        cs_p, valid_p = make_tables((c % 2) * TQ, 8192, TK)
        in_maps.append({
            "xT": np.stack([sg[0] for sg in segs]),
            "xn": np.stack([sg[1] for sg in segs]),
            "w_in_p": w_in_p,
            "w_br_a": np.ascontiguousarray(np.asarray(w_branch_a, np.float32)[0]),
            "w_br_b": np.ascontiguousarray(np.asarray(w_branch_b, np.float32)[0]),
            "w_out": np.ascontiguousarray(np.asarray(w_out, np.float32)[0]),
            "cs": np.stack([cs_s, cs_p]),
            "valid": np.stack([valid_s, valid_p]),
            "zrows": np.zeros((HALO, 512), np.float32),
            "masks": masks, "bgate": hb, "sinkrep": sinkrep, "lng": lng, "lnb": lnb,
        })
    res = run_bass_kernel_spmd(nc, in_maps, core_ids=list(range(NCORES)))
    y_prompt = np.empty((4, 8192, D), np.float32)
    y_sample = np.empty((16, 4096, D), np.float32)
    for c in range(NCORES):
        y = res.results[c]["y"]
        y_sample[2 * c] = y[0]
        y_sample[2 * c + 1] = y[1]
        y_prompt[c // 2, (c % 2) * TQ:(c % 2 + 1) * TQ] = y[2]
    return (y_prompt, y_sample)
```

```python
import math
from contextlib import ExitStack

import numpy as np
import concourse.bass as bass
import concourse.mybir as mybir
from concourse.bass_utils import run_bass_kernel_spmd

F32 = mybir.dt.float32
BF16 = mybir.dt.bfloat16
AF = mybir.ActivationFunctionType
ALU = mybir.AluOpType

D = 1024
HALO = 1024
NCORES = 8
LN_EPS = 1e-5
ALPHA = 2.0 ** 0.25
ROPE_THETA = 10000.0
GROUP_DILS = (1, 4, 16)

C_QA, C_KA, C_VA, C_GA = 0, 512, 1024, 1536
C_B = 2048
C_QB, C_KB, C_VB, C_GB = 2048, 2560, 2816, 2944
C_PRE = 3456
NW = C_PRE + 2048


class Sem:
    def __init__(self, nc, es, name):
        self.h = es.enter_context(nc.semaphore(name))
        self.n = 0


class Buf:
    __slots__ = ("lw", "rd", "dsem")

    def __init__(self, dsem=None):
        self.lw = None
        self.rd = []
        self.dsem = dsem


class Eng:
    def __init__(self, name, sem):
        self.name = name
        self.sem = sem
        self.prog = []
        self.waited = {}


class Prog:
    def __init__(self, nc, es):
        self.nc = nc
        self.es = es
        self.PE = Eng("pe", Sem(nc, es, "s_pe"))
        self.ACT = Eng("act", Sem(nc, es, "s_act"))
        self.DVE = Eng("dve", Sem(nc, es, "s_dve"))
        self.POOL = Eng("pool", Sem(nc, es, "s_pool"))
        self.SP = Eng("sp", Sem(nc, es, "s_sp"))
        self.nsem = 0
        self.dsems = []
        self.free_dsems = {"sw": [], "hw": []}
        self.kind = {}

    def newbuf(self, dma=False):
        if dma:
            kind = dma if isinstance(dma, str) else "hw"
            if self.free_dsems[kind]:
                return Buf(self.free_dsems[kind].pop())
            s = Sem(self.nc, self.es, "d%d" % self.nsem)
            self.nsem += 1
            self.dsems.append(s)
            self.kind[s] = kind
            return Buf(s)
        return Buf()

    def release(self, sems):
        for sm in sems:
            self.free_dsems[self.kind[sm]].append(sm)

    def _waits(self, E, reads, writes):
        waits = {}

        def need(t):
            if t is None:
                return
            sm, v = t
            if sm is E.sem and E.name == "pe":
                return
            if waits.get(sm, 0) < v:
                waits[sm] = v

        for b in reads:
            need(b.lw)
        for b in writes:
            need(b.lw)
            for t in b.rd:
                need(t)
        wl = [(sm, v) for sm, v in waits.items() if E.waited.get(sm, 0) < v]
        for sm, v in wl:
            E.waited[sm] = v
        return wl

    def op(self, E, fn, reads=(), writes=()):
        wl = self._waits(E, reads, writes)
        E.sem.n += 1
        tick = (E.sem, E.sem.n)
        E.prog.append((fn, wl, (E.sem, 1)))
        for b in reads:
            b.rd.append(tick)
        for b in writes:
            b.lw = tick
            b.rd = []
        return tick

    def dma(self, Q, out, in_, slot, reads=(), writes=()):
        wl = self._waits(Q, reads, writes)
        sm = slot.dsem
        assert self.kind[sm] == ("sw" if Q is self.POOL else "hw"), "semaphore shared between SW and HW DGE"
        sm.n += 16
        tick = (sm, sm.n)
        Q.prog.append((lambda e, o=out, i=in_: e.dma_start(out=o, in_=i), wl, (sm, 16)))
        for b in reads:
            b.rd.append(tick)
        for b in writes:
            b.lw = tick
            b.rd = []
        return tick

    def barrier(self, dsems=()):
        engs = [self.PE, self.ACT, self.DVE, self.POOL, self.SP]
        targets = [(e.sem, e.sem.n) for e in engs if e.sem.n > 0]
        targets += [(s, s.n) for s in dsems if s.n > 0]
        for E in engs:
            wl = [(sm, v) for sm, v in targets if sm is not E.sem and E.waited.get(sm, 0) < v]
            for sm, v in wl:
                E.waited[sm] = v
            if wl:
                E.prog.append((None, wl, None))

    def finish(self):
        nc = self.nc

        def run(E, e):
            for fn, wl, inc in E.prog:
                for sm, v in wl:
                    e.wait_ge(sm.h, v)
                if fn is None:
                    continue
                ins = fn(e)
                if inc is not None:
                    ins.then_inc(inc[0].h, inc[1])

        with nc.Block() as blk:
            @blk.tensor
            def _(e):
                run(self.PE, e)

            @blk.scalar
            def _(e):
                run(self.ACT, e)

            @blk.vector
            def _(e):
                run(self.DVE, e)

            @blk.gpsimd
            def _(e):
                run(self.POOL, e)

            @blk.sync
            def _(e):
                run(self.SP, e)


class Rot:
    def __init__(self, items):
        self.items = items
        self.i = 0

    def next(self):
        it = self.items[self.i % len(self.items)]
        self.i += 1
        return it


def build_program(seg_types, TQ, seg_halo=None):
    NSEG = len(seg_types)
    if seg_halo is None:
        seg_halo = [True] * NSEG
    NTYPE = max(seg_types) + 1
    TK = TQ + 2 * HALO
    NT = TK // 512
    NQT = TQ // 512
    QT0 = HALO // 512

    nc = bass.Bass("TRN2", target_bir_lowering=False)

    def din(name, shape, dt=F32):
        return nc.dram_tensor(name, list(shape), dt, kind="ExternalInput").ap()

    xT_d = din("xT", [NSEG, D, TK])
    xn_d = din("xn", [NSEG, TQ, D])
    w_d = din("w_in_p", [D, NW])
    wba_d = din("w_br_a", [512, D])
    wbb_d = din("w_br_b", [512, D])
    wo_d = din("w_out", [D, D])
    cs_d = din("cs", [NTYPE, 128, 2, TK])
    valid_d = din("valid", [NTYPE, TK, 64])
    masks_d = din("masks", [128, 1280])
    hb_d = din("bgate", [128, 16])
    sink_d = din("sinkrep", [128, 4])
    zrows_d = din("zrows", [HALO, 512])
    lng_d = din("lng", [128, D])
    lnb_d = din("lnb", [128, D])
    y_d = nc.dram_tensor("y", [NSEG, TQ, D], F32, kind="ExternalOutput").ap()

    va_d = nc.dram_tensor("va_scr", [NSEG, TK, 576], BF16, kind="Internal").ap()
    vb_d = nc.dram_tensor("vb_scr", [NSEG, TK, 192], BF16, kind="Internal").ap()
    yg_d = nc.dram_tensor("yg_scr", [NSEG, 2, 512, TQ], BF16, kind="Internal").ap()
    ya_d = nc.dram_tensor("ya_scr", [NSEG, 2, 512, TQ], BF16, kind="Internal").ap()

    dscr_d = nc.dram_tensor("dscr", [2, 2, TQ], F32, kind="Internal").ap()
    es = ExitStack()
    P = Prog(nc, es)
    PE, ACT, DVE, POOL, SP = P.PE, P.ACT, P.DVE, P.POOL, P.SP

    SZ_W = 8 * 2048 * 2
    SZ_WBO = (4 + 4 + 8) * 1024 * 2
    SZ_QT = 4 * TQ * 2
    SZ_KT = 4 * TK * 2
    SZ_CONST = 4096
    SZ_SH = 57344
    OFF_W = 0
    OFF_WBO = OFF_W + SZ_W
    OFF_CONST = OFF_WBO + SZ_WBO
    OFF_QT = OFF_CONST + SZ_CONST
    OFF_KT = OFF_QT + SZ_QT
    OFF_SH = OFF_KT + SZ_KT
    TOTAL = OFF_SH + SZ_SH
    arena = es.enter_context(nc.sbuf_tensor("arena", [128, TOTAL // 2], BF16))
    arena32 = arena.bitcast(F32)
    psum = es.enter_context(nc.psum_tensor("psum", [128, 8, 512], F32))

    def v16(off, n):
        assert off % 2 == 0
        return arena[:, off // 2: off // 2 + n]

    def v32(off, n):
        assert off % 4 == 0
        return arena32[:, off // 4: off // 4 + n]

    W = v16(OFF_W, 8 * 2048).rearrange("p (k n) -> p k n", k=8)
    WBA = v16(OFF_WBO, 4 * 1024).rearrange("p (k n) -> p k n", k=4)
    WBB = v16(OFF_WBO + 8192, 4 * 1024).rearrange("p (k n) -> p k n", k=4)
    WO = v16(OFF_WBO + 16384, 8 * 1024).rearrange("p (k n) -> p k n", k=8)
    QT = v16(OFF_QT, 4 * TQ).rearrange("p (c t) -> p c t", c=4)
    KT = v16(OFF_KT, 4 * TK).rearrange("p (c t) -> p c t", c=4)
    MASK = v16(OFF_CONST, 768)
    MBA = MASK[:, 0:256]
    MBB = MASK[:, 256:640]
    IDENT = MASK[:, 640:768]
    HB = v32(OFF_CONST + 2048, 16)
    ES = v32(OFF_CONST + 2048 + 64, 4)
    SINK = v32(OFF_CONST + 2048 + 64 + 16, 4)
    MHALF = v32(OFF_CONST + 2048 + 128, 1)
    ST8 = v32(OFF_CONST + 3072, 64)
    DROW = v32(OFF_CONST + 3072 + 256, 2 * (TQ // 128)).rearrange("p (h f) -> p h f", h=2)

    b_W = P.newbuf(dma="sw")
    b_WBO = P.newbuf(dma="sw")
    b_const = P.newbuf(dma="hw")
    b_mask = P.newbuf(dma="sw")
    b_QT = [[Buf() for _ in range(NQT)] for _ in range(4)]
    b_KT = [[Buf() for _ in range(NT)] for _ in range(4)]
    banks = [Buf() for _ in range(8)]

    b_va = [[Buf() for _ in range(NT)] for _ in range(NSEG)]
    b_vb = [[Buf() for _ in range(NT)] for _ in range(NSEG)]
    b_vva = [P.newbuf(dma="sw") for _ in range(NSEG)]
    b_vvb = [P.newbuf(dma="sw") for _ in range(NSEG)]
    b_yg = [[[Buf() for _ in range(4)] for _ in range(2)] for _ in range(NSEG)]
    b_ya = [[[Buf() for _ in range(4)] for _ in range(2)] for _ in range(NSEG)]

    wv = w_d.rearrange("(k p) n -> p k n", p=128)

    def load_W(c0, ncols):
        for k in range(8):
            P.dma(POOL, W[:, k, 0:ncols], wv[:, k, c0:c0 + ncols], b_W, writes=[b_W])

    load_W(0, 2048)
    P.dma(POOL, MASK, masks_d[:, 0:768], b_mask, writes=[b_mask])
    P.dma(SP, HB, hb_d, b_const, writes=[b_const])
    P.dma(SP, SINK, sink_d, b_const, writes=[b_const])
    P.dma(POOL, WBA, wba_d.rearrange("(k p) n -> p k n", p=128), b_WBO, writes=[b_WBO])
    P.dma(POOL, WBB, wbb_d.rearrange("(k p) n -> p k n", p=128), b_WBO, writes=[b_WBO])
    for k in range(8):
        P.dma(POOL, WO[:, k, :], wo_d[k * 128:(k + 1) * 128, :], b_WBO, writes=[b_WBO])
    for s in range(NSEG):
        P.dma(POOL, va_d[s, :, 512:576], valid_d[seg_types[s]], b_vva[s], writes=[b_vva[s]])
        P.dma(POOL, vb_d[s, :, 128:192], valid_d[seg_types[s]], b_vvb[s], writes=[b_vvb[s]])
    halo_cts = [ct for ct in range(NT) if not (QT0 <= ct < QT0 + NQT)]
    for s in range(NSEG):
        if seg_halo[s]:
            continue
        for (lo, cts) in ((0, [ct for ct in halo_cts if ct < QT0]), (HALO + TQ, [ct for ct in halo_cts if ct >= QT0])):
            P.dma(POOL, va_d[s, lo:lo + HALO, 0:512], zrows_d, b_vva[s], writes=[b_va[s][ct] for ct in cts])
            P.dma(POOL, vb_d[s, lo:lo + HALO, 0:128], zrows_d[:, 0:128], b_vvb[s], writes=[b_vb[s][ct] for ct in cts])
    P.op(ACT, lambda e: e.activation(out=ES, in_=SINK, func=AF.Exp), reads=[b_const], writes=[b_const])
    P.op(DVE, lambda e: e.memset(MHALF, -0.5), writes=[b_const])
    P.op(DVE, lambda e: e.tensor_scalar(out=HB, in0=HB, scalar1=0.5, scalar2=None, op0=ALU.mult), reads=[b_const], writes=[b_const])

    def phase_proj(s, mixer):
        typ = seg_types[s]
        o = OFF_SH
        XS = [v16(o + i * 8192, 8 * 512).rearrange("p (k t) -> p k t", k=8) for i in range(2)]
        o += 16384
        TB = [v32(o + i * 4096, 1024).rearrange("p (a t) -> p a t", a=2) for i in range(2)]
        o += 8192
        TMPA = [v32(o + i * 2048, 512) for i in range(2)]
        o += 4096
        TMPB = [v32(o + i * 2048, 512) for i in range(2)]
        o += 4096
        STG = [v16(o + i * 1024, 512) for i in range(8)]
        o += 8192
        XF = [v32(o + i * 8192, 4 * 512).rearrange("p (k t) -> p k t", k=4) for i in range(2)]
        o += 16384
        assert o <= OFF_SH + SZ_SH
        xs_b = Rot([(XS[i], Buf()) for i in range(2)])
        xf_b = [(XF[i], P.newbuf(dma=True)) for i in range(2)]
        tb_b = Rot([(TB[i], P.newbuf(dma=True)) for i in range(2)])
        ta_b = Rot([(TMPA[i], Buf()) for i in range(2)])
        tbb_b = Rot([(TMPB[i], Buf()) for i in range(2)])
        stg_b = Rot([(STG[i], P.newbuf(dma=True)) for i in range(8)])
        bk = Rot([(psum[:, i, :], banks[i]) for i in range(8)])
        local_dsems = [b.dsem for _, b in xf_b + tb_b.items + stg_b.items]

        if mixer == 0:
            cq, ck, cv, cg = 0, 512, 1024, 1536
            nk, vw = 4, 512
            v_scr, bv = va_d, b_va
        else:
            cq, ck, cv, cg = 0, 512, 768, 896
            nk, vw = 2, 128
            v_scr, bv = vb_d, b_vb

        xv = xT_d[s].rearrange("(k p) t -> p k t", p=128)
        shuf = [i ^ 1 for i in range(32)]

        def xload(ct):
            for hh in range(2):
                xf, xfb = xf_b[hh]
                P.dma(SP, xf, xv[:, 4 * hh:4 * hh + 4, ct * 512:(ct + 1) * 512], xfb, writes=[xfb])

        tbl = {}

        def tload(ct):
            tb, tbuf = tb_b.next()
            P.dma(SP, tb, cs_d[typ, :, :, ct * 512:(ct + 1) * 512], tbuf, writes=[tbuf])
            tbl[ct] = (tb, tbuf)

        def do_cast():
            xs, xb = xs_b.next()
            for hh in range(2):
                xf, xfb = xf_b[hh]
                P.op(ACT, lambda e, xs=xs, xf=xf, hh=hh: e.activation(out=xs[:, 4 * hh:4 * hh + 4, :], in_=xf, func=AF.Copy), reads=[xfb], writes=[xb])
            return xs, xb

        if seg_halo[s]:
            cts = list(range(NT))
        else:
            cts = [ct for ct in range(NT) if QT0 <= ct < QT0 + NQT]
            for (lo, hc) in ((0, [ct for ct in halo_cts if ct < QT0]), (HALO + TQ, [ct for ct in halo_cts if ct >= QT0])):
                P.op(POOL, lambda e, lo=lo: e.memset(KT[:, 0:nk, lo:lo + HALO], 0.0), writes=[b_KT[c][ct] for c in range(nk) for ct in hc])
        xload(cts[0])
        tload(cts[0])
        nxt = do_cast()
        for ci_, ct in enumerate(cts):
            c0 = ct * 512
            isq = QT0 <= ct < QT0 + NQT
            xs, xb = nxt
            tb, tbuf = tbl.pop(ct)
            if ci_ + 1 < len(cts):
                xload(cts[ci_ + 1])
                tload(cts[ci_ + 1])

            def proj(col, xs=xs):
                ps, pb = bk.next()

                def f(e, ps=ps, col=col, xs=xs):
                    for k in range(8):
                        ins = e.matmul(ps, lhsT=W[:, k, col:col + 128], rhs=xs[:, k, :], start=(k == 0), stop=(k == 7))
                    return ins
                P.op(PE, f, reads=[b_W, xb], writes=[pb])
                return ps, pb

            def rope(ps, pb, dst, dbuf, tb=tb, tbuf=tbuf):
                ta, tab = ta_b.next()
                t2, t2b = tbb_b.next()
                P.op(DVE, lambda e: e.stream_shuffle(out=ta, in_=ps, mask=shuf), reads=[pb], writes=[tab])
                P.op(DVE, lambda e: e.tensor_tensor(out=t2, in0=ps, in1=tb[:, 0, :], op=ALU.mult), reads=[pb, tbuf], writes=[t2b])
                P.op(DVE, lambda e: e.tensor_tensor(out=ta, in0=ta, in1=tb[:, 1, :], op=ALU.mult), reads=[tab, tbuf], writes=[tab])
                P.op(DVE, lambda e: e.tensor_tensor(out=dst, in0=t2, in1=ta, op=ALU.add), reads=[tab, t2b], writes=[dbuf])

            for c in range(nk):
                ps, pb = proj(ck + c * 128)
                rope(ps, pb, KT[:, c, c0:c0 + 512], b_KT[c][ct])
            if isq:
                qi = ct - QT0
                q0 = qi * 512
                for c in range(4):
                    ps, pb = proj(cq + c * 128)
                    rope(ps, pb, QT[:, c, q0:q0 + 512], b_QT[c][qi])
                for c in range(4):
                    ps, pb = proj(cg + c * 128)
                    ta, tab = ta_b.next()
                    sg, sgb = stg_b.next()
                    P.op(ACT, lambda e, ta=ta, ps=ps: e.activation(out=ta, in_=ps, func=AF.Tanh, scale=0.5), reads=[pb], writes=[tab])
                    P.op(DVE, lambda e, ta=ta, ps=ps, sg=sg: e.scalar_tensor_tensor(out=sg, in0=ta, scalar=1.0, in1=ps, op0=ALU.add, op1=ALU.mult),
                         reads=[tab, pb], writes=[sgb])
                    P.dma(SP, yg_d[s, mixer, c * 128:(c + 1) * 128, q0:q0 + 512], sg, sgb, reads=[sgb], writes=[b_yg[s][mixer][c]])
            if ci_ + 1 < len(cts):
                nxt = do_cast()
            for tt in range(4):
                ps, pb = bk.next()

                def f(e, ps=ps, xs=xs, tt=tt):
                    for k in range(8):
                        ins = e.matmul(ps[:, 0:vw], lhsT=xs[:, k, tt * 128:(tt + 1) * 128], rhs=W[:, k, cv:cv + vw], start=(k == 0), stop=(k == 7))
                    return ins
                P.op(PE, f, reads=[b_W, xb], writes=[pb])
                sg, sgb = stg_b.next()
                P.op(ACT, lambda e, sg=sg, ps=ps: e.activation(out=sg[:, 0:vw], in_=ps[:, 0:vw], func=AF.Copy), reads=[pb], writes=[sgb])
                r0 = c0 + tt * 128
                P.dma(SP, v_scr[s, r0:r0 + 128, 0:vw], sg[:, 0:vw], sgb, reads=[sgb], writes=[bv[s][ct]])
        if mixer == 0:
            load_W(C_B, 1408)
        else:
            load_W(C_PRE, 2048)
        P.barrier(local_dsems)
        P.release(local_dsems)

    def phase_attn(s, mixer):
        o = OFF_SH
        ACCO = v32(o, TQ)
        o += TQ * 4
        ACCD = v32(o, TQ)
        o += TQ * 4
        SGP = v16(o, TQ)
        o += TQ * 2
        YP = SGP
        PT = [v16(o + i * 1536, 768).rearrange("p (h q) -> p h q", h=2) for i in range(3)]
        o += 3 * 1536
        vwid = 576 if mixer == 0 else 192
        NVT = 6
        VT = [v16(o + i * 1152, vwid) for i in range(NVT)]
        o += NVT * 1152
        MULM = v16(o, 512).rearrange("p (r h q) -> p r h q", r=2, h=2)
        o += 4096
        assert o <= OFF_SH + SZ_SH, (o, OFF_SH + SZ_SH)
        ACH = 1024
        b_accs = [Buf() for _ in range(TQ // ACH)]
        b_rscr = Buf()
        b_fin = P.newbuf(dma="sw")
        b_mulm = P.newbuf(dma="sw")
        b_drows = [Buf(), Buf()]
        b_dscr = [Buf(), Buf()]
        pending = []

        def flush():
            while pending:
                pending.pop(0)()
        b_sgp = P.newbuf(dma=True)
        b_yp = b_sgp
        pt_b = Rot([(PT[i], Buf()) for i in range(3)])
        vt_b = Rot([(VT[i], P.newbuf(dma=True)) for i in range(NVT)])
        local_dsems = [b_sgp.dsem, b_fin.dsem, b_mulm.dsem] + [b.dsem for _, b in vt_b.items]
        scale = 1.0 / 8.0
        st_cnt = [0]
        bf_cnt = [0]

        if mixer == 0:
            v_scr, bv, bvv = va_d, b_va, b_vva[s]
            chains = []
            for d in GROUP_DILS:
                for r in range(d):
                    chains.append((d, r))
            roles = ((1, 1), (0, 0))
            ones_c0 = 512
        else:
            v_scr, bv, bvv = vb_d, b_vb, b_vvb[s]
            chains = [(1, 0)]
            roles = ((2, 1), (1, None), (0, 0))
            ones_c0 = 128
        maxdb = max(r[0] for r in roles)

        for pair in range(4):
            kc = pair if mixer == 0 else pair // 2
            first_chain = True
            for (d, r) in chains:
                NB = TQ // (128 * d)
                if mixer == 0:
                    kbase = r + HALO - 64 * d
                else:
                    kbase = HALO - 128
                for B0 in range(0, NB, 4):
                    nb = min(4, NB - B0)
                    j = bf_cnt[0] % 2
                    bf_cnt[0] += 1
                    OB, ob = psum[:, 2 + 2 * j, :], banks[2 + 2 * j]
                    DB, db_ = psum[:, 3 + 2 * j, :], banks[3 + 2 * j]
                    started = [False, False]
                    for t in range(B0, B0 + nb + maxdb):
                        served = []
                        for (dbk, role) in roles:
                            b = t - dbk
                            if B0 <= b < B0 + nb:
                                served.append((b - B0, role))
                        if not served:
                            continue
                        n = 128 * len(served)
                        rel0 = kbase + 128 * d * t
                        vt, vtb = vt_b.next()
                        tiles = set(range(rel0 // 512, (rel0 + 127 * d) // 512 + 1))
                        P.dma(SP, vt, v_scr[s, rel0:rel0 + 127 * d + 1:d, :], vtb,
                              reads=[bv[s][x] for x in tiles] + [bvv], writes=[vtb])
                        half = st_cnt[0] % 2
                        st_cnt[0] += 1
                        pt, ptb = pt_b.next()
                        kreads = [b_KT[kc][x] for x in tiles]
                        qreads = []
                        for (cb, role) in served:
                            q0 = r + 128 * d * (B0 + cb)
                            for x in range(q0 // 512, (q0 + 127 * d) // 512 + 1):
                                if b_QT[pair][x] not in qreads:
                                    qreads.append(b_QT[pair][x])

                        sb0 = 0 if half == 0 else 6

                        if mixer == 0:
                            mcol = {1: 0, 0: 128}
                            MB = MBA
                        else:
                            mcol = {1: 0, None: 128, 0: 256}
                            MB = MBB
                        m0 = mcol[served[0][1]]
                        need_mask = any(role is not None for _, role in served)

                        def fqk(e, served=served, rel0=rel0, sb0=sb0, d=d, r=r, B0=B0, kc=kc, pair=pair, n=n, m0=m0, MB=MB, need_mask=need_mask):
                            ins = None
                            q0 = r + 128 * d * (B0 + served[0][0])
                            for h in range(2):
                                ins = e.matmul(psum[:, sb0 + h, 0:n],
                                               lhsT=KT[64 * h:64 * h + 64, kc, rel0:rel0 + 127 * d + 1:d],
                                               rhs=QT[64 * h:64 * h + 64, pair, q0:q0 + (n - 1) * d + 1:d],
                                               start=True, stop=not need_mask)
                            if need_mask:
                                for h in range(2):
                                    ins = e.matmul(psum[:, sb0 + h, 0:n], lhsT=IDENT, rhs=MB[:, m0:m0 + n], start=False, stop=True,
                                                   skip_group_check=True)
                            return ins
                        P.op(PE, fqk, reads=kreads + qreads + [b_mask], writes=[banks[sb0], banks[sb0 + 1]])

                        P.op(ACT, lambda e, pt=pt, n=n, sb0=sb0: e.activation(out=pt[:, :, 0:n], in_=psum[:, sb0:sb0 + 2, 0:n], func=AF.Exp, scale=scale),
                             reads=[banks[sb0], banks[sb0 + 1]], writes=[ptb])

                        st_flags = [not started[0], not started[1]]
                        started[0] = started[1] = True

                        def fpv(e, served=served, pt=pt, vt=vt, OB=OB, DB=DB, pair=pair, st_flags=st_flags, n=n):
                            ins = None
                            c0 = served[0][0] * 128
                            for h in range(2):
                                if mixer == 0:
                                    vcol = (2 * pair + h) * 64
                                else:
                                    vcol = (pair // 2) * 64
                                e.matmul(OB[64 * h:64 * h + 64, c0:c0 + n], lhsT=vt[:, vcol:vcol + 64],
                                         rhs=pt[:, h, 0:n], start=st_flags[h], stop=False, skip_group_check=True)
                                ins = e.matmul(DB[64 * h:64 * h + 64, c0:c0 + n], lhsT=vt[:, ones_c0:ones_c0 + 64],
                                               rhs=pt[:, h, 0:n], start=st_flags[h], stop=False, skip_group_check=True)
                            return ins
                        flush()
                        pending.append(lambda fpv=fpv, ptb=ptb, vtb=vtb, ob=ob, db_=db_: P.op(PE, fpv, reads=[ptb, vtb], writes=[ob, db_]))
                    qs = r + 128 * d * B0
                    qe = qs + (128 * nb - 1) * d + 1
                    nn = 128 * nb
                    accb = [b_accs[x] for x in range(qs // ACH, (qe - 1) // ACH + 1)]

                    def evac(qs=qs, qe=qe, d=d, nn=nn, OB=OB, DB=DB, ob=ob, db_=db_, accb=accb, fc=first_chain, pair=pair):
                        if fc:
                            P.op(ACT, lambda e: e.activation(out=ACCO[:, qs:qe:d], in_=OB[:, 0:nn], func=AF.Copy), reads=[ob], writes=accb)
                            if mixer == 1:
                                P.op(DVE, lambda e: e.tensor_scalar(out=ACCD[:, qs:qe:d], in0=DB[:, 0:nn], scalar1=ES[:, pair:pair + 1], scalar2=None, op0=ALU.add),
                                     reads=[db_, b_const], writes=accb)
                            else:
                                P.op(DVE, lambda e: e.tensor_copy(out=ACCD[:, qs:qe:d], in_=DB[:, 0:nn]), reads=[db_], writes=accb)
                        else:
                            P.op(DVE, lambda e: e.tensor_tensor(out=ACCO[:, qs:qe:d], in0=OB[:, 0:nn], in1=ACCO[:, qs:qe:d], op=ALU.add),
                                 reads=[ob], writes=accb)
                            P.op(DVE, lambda e: e.tensor_tensor(out=ACCD[:, qs:qe:d], in0=DB[:, 0:nn], in1=ACCD[:, qs:qe:d], op=ALU.add),
                                 reads=[db_], writes=accb)
                    pending.append(evac)
                first_chain = False
            flush()
            P.dma(SP, SGP, yg_d[s, mixer, pair * 128:(pair + 1) * 128, :], b_sgp, reads=[b_yg[s][mixer][pair]], writes=[b_sgp])
            for h in range(2):
                P.dma(POOL, DROW[:, h, :], ACCD[64 * h:64 * h + 1, :], b_fin, reads=b_accs, writes=[b_drows[h]])
            P.op(DVE, lambda e: e.reciprocal(out=DROW, in_=DROW), reads=[], writes=b_drows)
            P.dma(POOL, dscr_d[1].rearrange("h (p f) -> p h f", p=128), DROW, b_fin, reads=b_drows, writes=[b_dscr[1]])
            for h in range(2):
                P.dma(POOL, ACCD[64 * h:64 * h + 64, :], dscr_d[1, h, :].partition_broadcast(64), b_fin, reads=[b_dscr[1]], writes=b_accs)
            CH = ACH
            for ci, c0 in enumerate(range(0, TQ, CH)):
                ab = [b_accs[ci]]
                P.op(DVE, lambda e, c0=c0: e.scalar_tensor_tensor(out=ACCO[:, c0:c0 + CH], in0=ACCO[:, c0:c0 + CH], scalar=0.5, in1=ACCD[:, c0:c0 + CH],
                                                                 op0=ALU.mult, op1=ALU.mult), reads=[], writes=ab)
                P.op(DVE, lambda e, c0=c0: e.tensor_tensor(out=YP[:, c0:c0 + CH], in0=ACCO[:, c0:c0 + CH], in1=SGP[:, c0:c0 + CH], op=ALU.mult),
                     reads=ab, writes=[b_sgp])
            P.dma(POOL, ya_d[s, mixer, pair * 128:(pair + 1) * 128, :], YP, b_fin, reads=[b_yp], writes=[b_ya[s][mixer][pair]])
        P.barrier(local_dsems)
        P.release(local_dsems)

    def phase_out(s):
        o = OFF_QT
        XS = [v16(o + i * 8192, 8 * 512).rearrange("p (k t) -> p k t", k=8) for i in range(2)]
        o += 16384
        YAB = [v16(o + i * 8192, 8 * 512).rearrange("p (k t) -> p k t", k=8) for i in range(2)]
        o += 16384
        G = v16(o, 16 * 512).rearrange("p (c t) -> p c t", c=16)
        o += 16384
        MG = v16(o, 8 * 512).rearrange("p (c t) -> p c t", c=8)
        o += 8192
        TMP = [v32(o + i * 2048, 512) for i in range(2)]
        o += 4096
        XR = [v32(o + i * 4096, 1024) for i in range(2)]
        o += 8192
        ZZ = [v32(o + i * 4096, 1024) for i in range(2)]
        o += 8192
        LNG = v32(o, 1024)
        o += 4096
        LNB = v32(o, 1024)
        o += 4096
        XF = [v32(o + i * 8192, 4 * 512).rearrange("p (k t) -> p k t", k=4) for i in range(2)]
        o += 16384
        assert o <= TOTAL
        xs_b = Rot([(XS[i], Buf()) for i in range(2)])
        xf_b = [(XF[i], P.newbuf(dma=True)) for i in range(2)]
        yab_b = Rot([(YAB[i], P.newbuf(dma=True)) for i in range(2)])
        tmp_b = Rot([(TMP[i], Buf()) for i in range(2)])
        xr_b = Rot([(XR[i], P.newbuf(dma=True)) for i in range(2)])
        zz_b = Rot([(ZZ[i], P.newbuf(dma="sw")) for i in range(2)])
        b_G = [Buf() for _ in range(16)]
        b_MG = [Buf() for _ in range(8)]
        b_ln = P.newbuf(dma=True)
        b_sts = [Buf(), Buf()]
        bk = Rot([(psum[:, i, :], banks[i]) for i in range(8)])
        local_dsems = [b.dsem for _, b in xf_b + yab_b.items + xr_b.items + zz_b.items] + [b_ln.dsem]

        P.dma(SP, LNG, lng_d, b_ln, writes=[b_ln])
        P.dma(SP, LNB, lnb_d, b_ln, writes=[b_ln])
        xv = xT_d[s].rearrange("(k p) t -> p k t", p=128)
        def xload(qi):
            c0 = HALO + qi * 512
            for hh in range(2):
                xf, xfb = xf_b[hh]
                P.dma(SP, xf, xv[:, 4 * hh:4 * hh + 4, c0:c0 + 512], xfb, writes=[xfb])

        ytl = {}

        def yload(qi):
            yab, yb = yab_b.next()
            for m in range(2):
                P.dma(SP, yab[:, 4 * m:4 * m + 4, :], ya_d[s, m].rearrange("(c p) t -> p c t", p=128)[:, :, qi * 512:(qi + 1) * 512], yb,
                      reads=b_ya[s][m], writes=[yb])
            ytl[qi] = (yab, yb)

        def do_cast():
            xs, xb = xs_b.next()
            for hh in range(2):
                xf, xfb = xf_b[hh]
                P.op(ACT, lambda e, xs=xs, xf=xf, hh=hh: e.activation(out=xs[:, 4 * hh:4 * hh + 4, :], in_=xf, func=AF.Copy), reads=[xfb], writes=[xb])
            return xs, xb

        xload(0)
        yload(0)
        nxt = do_cast()
        for qi in range(NQT):
            q0 = qi * 512
            xs, xb = nxt
            yab, yb = ytl.pop(qi)
            if qi + 1 < NQT:
                xload(qi + 1)
                yload(qi + 1)
            for cg in range(16):
                ps, pb = bk.next()

                def f(e, ps=ps, cg=cg, xs=xs):
                    for k in range(8):
                        ins = e.matmul(ps, lhsT=W[:, k, cg * 128:(cg + 1) * 128], rhs=xs[:, k, :], start=(k == 0), stop=(k == 7))
                    return ins
                P.op(PE, f, reads=[b_W, xb], writes=[pb])
                P.op(ACT, lambda e, ps=ps, cg=cg: e.activation(out=G[:, cg, :], in_=ps, func=AF.Tanh, scale=0.5, bias=HB[:, cg:cg + 1]),
                     reads=[pb, b_const], writes=[b_G[cg]])
            if qi + 1 < NQT:
                nxt = do_cast()
            for dc in range(8):
                psa, pba = bk.next()
                psb, pbb = bk.next()

                def fa(e, psa=psa, dc=dc, yab=yab):
                    for k in range(4):
                        ins = e.matmul(psa, lhsT=WBA[:, k, dc * 128:(dc + 1) * 128], rhs=yab[:, k, :], start=(k == 0), stop=(k == 3))
                    return ins

                def fb(e, psb=psb, dc=dc, yab=yab):
                    for k in range(4):
                        ins = e.matmul(psb, lhsT=WBB[:, k, dc * 128:(dc + 1) * 128], rhs=yab[:, 4 + k, :], start=(k == 0), stop=(k == 3))
                    return ins
                P.op(PE, fa, reads=[b_WBO, yb], writes=[pba])
                P.op(PE, fb, reads=[b_WBO, yb], writes=[pbb])
                t1, t1b = tmp_b.next()
                t2, t2b = tmp_b.next()
                P.op(DVE, lambda e, t1=t1, psa=psa, dc=dc: e.scalar_tensor_tensor(out=t1, in0=G[:, dc, :], scalar=1.0, in1=psa, op0=ALU.add, op1=ALU.mult),
                     reads=[pba, b_G[dc]], writes=[t1b])
                P.op(DVE, lambda e, t2=t2, psb=psb, dc=dc: e.scalar_tensor_tensor(out=t2, in0=G[:, 8 + dc, :], scalar=1.0, in1=psb, op0=ALU.add, op1=ALU.mult),
                     reads=[pbb, b_G[8 + dc]], writes=[t2b])
                P.op(DVE, lambda e, t1=t1, t2=t2, dc=dc: e.tensor_tensor(out=MG[:, dc, :], in0=t1, in1=t2, op=ALU.add),
                     reads=[t1b, t2b], writes=[b_MG[dc]])
            for tt in range(4):
                r0 = q0 + tt * 128
                sb = 32 * (tt % 2)
                bst = b_sts[tt % 2]
                xr, xrb = xr_b.next()
                P.dma(SP, xr, xn_d[s, r0:r0 + 128, :], xrb, writes=[xrb])
                zz, zzb = zz_b.next()
                P.op(ACT, lambda e, xr=xr: e.activation(out=xr, in_=xr, func=AF.Copy, scale=ALPHA), reads=[xrb], writes=[xrb])
                for hf in range(2):
                    ps, pb = bk.next()

                    def f(e, ps=ps, tt=tt, hf=hf):
                        for k in range(8):
                            ins = e.matmul(ps, lhsT=MG[:, k, tt * 128:(tt + 1) * 128], rhs=WO[:, k, hf * 512:(hf + 1) * 512], start=(k == 0), stop=(k == 7))
                        return ins
                    P.op(PE, f, reads=[b_WBO] + b_MG, writes=[pb])
                    P.op(DVE, lambda e, zz=zz, ps=ps, xr=xr, hf=hf: e.scalar_tensor_tensor(out=zz[:, hf * 512:(hf + 1) * 512], in0=ps, scalar=0.5,
                                                                                          in1=xr[:, hf * 512:(hf + 1) * 512], op0=ALU.mult, op1=ALU.add),
                         reads=[pb, xrb], writes=[zzb])
                    P.op(DVE, lambda e, zz=zz, hf=hf, sb=sb: e.bn_stats(out=ST8[:, sb + hf * 6:sb + (hf + 1) * 6], in_=zz[:, hf * 512:(hf + 1) * 512]),
                         reads=[zzb], writes=[bst])
                P.op(DVE, lambda e, sb=sb: e.bn_aggr(out=ST8[:, sb + 12:sb + 14], in_=ST8[:, sb + 0:sb + 12]), reads=[bst], writes=[bst])
                P.op(DVE, lambda e, sb=sb: e.tensor_scalar(out=ST8[:, sb + 14:sb + 15], in0=ST8[:, sb + 13:sb + 14], scalar1=LN_EPS, scalar2=None, op0=ALU.add), reads=[bst], writes=[bst])
                P.op(POOL, lambda e, sb=sb: e.tensor_tensor(out=ST8[:, sb + 15:sb + 16], in0=ST8[:, sb + 14:sb + 15], in1=MHALF, op=ALU.pow), reads=[bst, b_const], writes=[bst])
                P.op(DVE, lambda e, sb=sb: e.tensor_scalar(out=ST8[:, sb + 16:sb + 17], in0=ST8[:, sb + 12:sb + 13], scalar1=ST8[:, sb + 15:sb + 16], scalar2=-1.0, op0=ALU.mult, op1=ALU.mult),
                     reads=[bst], writes=[bst])
                P.op(ACT, lambda e, zz=zz, sb=sb: e.activation(out=zz, in_=zz, func=AF.Identity, scale=ST8[:, sb + 15:sb + 16], bias=ST8[:, sb + 16:sb + 17]),
                     reads=[zzb, bst], writes=[zzb])
                P.op(DVE, lambda e, zz=zz: e.tensor_tensor(out=zz, in0=zz, in1=LNG, op=ALU.mult), reads=[zzb, b_ln], writes=[zzb])
                P.op(POOL, lambda e, zz=zz: e.tensor_tensor(out=zz, in0=zz, in1=LNB, op=ALU.add), reads=[zzb, b_ln], writes=[zzb])
                P.dma(POOL, y_d[s, r0:r0 + 128, :], zz, zzb, reads=[zzb])
        if s + 1 < NSEG:
            load_W(0, 2048)
        P.barrier(local_dsems)
        P.release(local_dsems)

    for s in range(NSEG):
        phase_proj(s, 0)
        phase_attn(s, 0)
        phase_proj(s, 1)
        phase_attn(s, 1)
        phase_out(s)
    P.barrier(P.dsems)
    P.finish()
    es.close()
    return nc


def _perm_cols_interleave(w, nheads):
    idx = np.empty(64, np.int64)
    idx[0::2] = np.arange(32)
    idx[1::2] = 32 + np.arange(32)
    cols = np.concatenate([h * 64 + idx for h in range(nheads)])
    return w[:, cols]


def make_w_in_p(w_in):
    qa, ka, va, ga = w_in[:, 0:512], w_in[:, 512:1024], w_in[:, 1024:1536], w_in[:, 1536:2048]
    qb, kb, vb, gb = w_in[:, 2048:2560], w_in[:, 2560:2688], w_in[:, 2688:2816], w_in[:, 2816:3328]
    pre = w_in[:, 3328:5376]
    kbp = _perm_cols_interleave(kb, 2)
    kbdup = np.concatenate([kbp[:, 0:64], kbp[:, 0:64], kbp[:, 64:128], kbp[:, 64:128]], axis=1)
    out = np.concatenate([_perm_cols_interleave(qa, 8), _perm_cols_interleave(ka, 8), va, ga,
                          _perm_cols_interleave(qb, 8), kbdup, vb, gb, pre], axis=1)
    assert out.shape[1] == NW
    return np.ascontiguousarray(out, dtype=np.float32)


def make_tables(start, seq_len, TK):
    pos = np.arange(TK, dtype=np.int64) - HALO + start
    half = 32
    inv_freq = (np.float32(ROPE_THETA) ** (-np.arange(half, dtype=np.float32) / np.float32(half))).astype(np.float32)
    ang = pos.astype(np.float32)[None, :] * inv_freq[:, None]
    cos = np.cos(ang.astype(np.float64)).astype(np.float32)
    sin = np.sin(ang.astype(np.float64)).astype(np.float32)
    rows = np.arange(128)
    fi = (rows % 64) // 2
    sign = np.where(rows % 2 == 0, -1.0, 1.0).astype(np.float32)
    cs = np.empty((128, 2, TK), np.float32)
    cs[:, 0, :] = cos[fi]
    cs[:, 1, :] = sin[fi] * sign[:, None]
    valid = ((pos >= 0) & (pos < seq_len)).astype(np.float32)
    valid = np.repeat(valid[:, None], 64, axis=1)
    return cs, np.ascontiguousarray(valid)


def make_consts(b_gate, sink_logit, ln_gain, ln_bias):
    k = np.arange(128)[:, None]
    q = np.arange(128)[None, :]
    mU = (k >= q).astype(np.float32)
    mL = (k <= q).astype(np.float32)
    NEG = np.float32(-30000.0)
    bU = np.where(mU > 0, np.float32(0), NEG).astype(np.float32)
    bL = np.where(mL > 0, np.float32(0), NEG).astype(np.float32)
    masks = np.concatenate([bL, bU, bL, np.zeros((128, 128), np.float32), bU, np.eye(128, dtype=np.float32)], axis=1)
    masks = np.ascontiguousarray(np.concatenate([masks, mU, mU, mL, mL], axis=1))
    hb = np.ascontiguousarray(b_gate.reshape(16, 128).T.astype(np.float32))
    sinkrep = np.empty((128, 4), np.float32)
    for p in range(4):
        sinkrep[0:64, p] = sink_logit[2 * p]
        sinkrep[64:128, p] = sink_logit[2 * p + 1]
    lng = np.ascontiguousarray(np.broadcast_to(ln_gain[None, :], (128, D)), dtype=np.float32)
    lnb = np.ascontiguousarray(np.broadcast_to(ln_bias[None, :], (128, D)), dtype=np.float32)
    return masks, hb, sinkrep, lng, lnb


def seg_inputs(x_seq, start, TQ):
    S = x_seq.shape[0]
    TK = TQ + 2 * HALO
    xT = np.zeros((D, TK), np.float32)
    lo = max(0, start - HALO)
    hi = min(S, start + TQ + HALO)
    xT[:, lo - (start - HALO):hi - (start - HALO)] = x_seq[lo:hi].T
    return xT, np.ascontiguousarray(x_seq[start:start + TQ])


_NC_CACHE = {}


def kernel(x_prompt, x_sample, w_in, b_gate, sink_logit, w_branch_a, w_branch_b, w_out, ln_gain, ln_bias):
    x_prompt = np.asarray(x_prompt, np.float32)
    x_sample = np.asarray(x_sample, np.float32)
    TQ = 4096
    TK = TQ + 2 * HALO
    w_in_p = make_w_in_p(np.asarray(w_in, np.float32)[0])
    masks, hb, sinkrep, lng, lnb = make_consts(np.asarray(b_gate, np.float32)[0], np.asarray(sink_logit, np.float32)[0],
                                               np.asarray(ln_gain, np.float32)[0], np.asarray(ln_bias, np.float32)[0])
    cs_s, valid_s = make_tables(0, 4096, TK)
    key = ("full",)
    if key not in _NC_CACHE:
        _NC_CACHE[key] = build_program([0, 0, 1], TQ, seg_halo=[False, False, True])
    nc = _NC_CACHE[key]
    in_maps = []
    for c in range(NCORES):
        segs = [seg_inputs(x_sample[2 * c], 0, TQ), seg_inputs(x_sample[2 * c + 1], 0, TQ),
                seg_inputs(x_prompt[c // 2], (c % 2) * TQ, TQ)]
        cs_p, valid_p = make_tables((c % 2) * TQ, 8192, TK)
        in_maps.append({
            "xT": np.stack([sg[0] for sg in segs]),
            "xn": np.stack([sg[1] for sg in segs]),
            "w_in_p": w_in_p,
            "w_br_a": np.ascontiguousarray(np.asarray(w_branch_a, np.float32)[0]),
            "w_br_b": np.ascontiguousarray(np.asarray(w_branch_b, np.float32)[0]),
            "w_out": np.ascontiguousarray(np.asarray(w_out, np.float32)[0]),
            "cs": np.stack([cs_s, cs_p]),
            "valid": np.stack([valid_s, valid_p]),
            "zrows": np.zeros((HALO, 512), np.float32),
            "masks": masks, "bgate": hb, "sinkrep": sinkrep, "lng": lng, "lnb": lnb,
        })
    res = run_bass_kernel_spmd(nc, in_maps, core_ids=list(range(NCORES)))
    y_prompt = np.empty((4, 8192, D), np.float32)
    y_sample = np.empty((16, 4096, D), np.float32)
    for c in range(NCORES):
        y = res.results[c]["y"]
        y_sample[2 * c] = y[0]
        y_sample[2 * c + 1] = y[1]
        y_prompt[c // 2, (c % 2) * TQ:(c % 2 + 1) * TQ] = y[2]
    return (y_prompt, y_sample)
```

```python
import math
from contextlib import ExitStack

import numpy as np
import concourse.bass as bass
import concourse.mybir as mybir
from concourse.bass_utils import run_bass_kernel_spmd

F32 = mybir.dt.float32
BF16 = mybir.dt.bfloat16
AF = mybir.ActivationFunctionType
ALU = mybir.AluOpType

D = 1024
HALO = 1024
NCORES = 8
LN_EPS = 1e-5
ALPHA = 2.0 ** 0.25
ROPE_THETA = 10000.0
GROUP_DILS = (1, 4, 16)
ABATCH = 2

C_QA, C_KA, C_VA, C_GA = 0, 512, 1024, 1536
C_B = 2048
C_QB, C_KB, C_VB, C_GB = 2048, 2560, 2816, 2944
C_PRE = 3456
NW = C_PRE + 2048


class Sem:
    def __init__(self, nc, es, name):
        self.h = es.enter_context(nc.semaphore(name))
        self.n = 0


class Buf:
    __slots__ = ("lw", "rd", "dsem")

    def __init__(self, dsem=None):
        self.lw = None
        self.rd = []
        self.dsem = dsem


class Eng:
    def __init__(self, name, sem):
        self.name = name
        self.sem = sem
        self.prog = []
        self.waited = {}


class Prog:
    def __init__(self, nc, es):
        self.nc = nc
        self.es = es
        self.PE = Eng("pe", Sem(nc, es, "s_pe"))
        self.ACT = Eng("act", Sem(nc, es, "s_act"))
        self.DVE = Eng("dve", Sem(nc, es, "s_dve"))
        self.POOL = Eng("pool", Sem(nc, es, "s_pool"))
        self.SP = Eng("sp", Sem(nc, es, "s_sp"))
        self.nsem = 0
        self.dsems = []
        self.free_dsems = {"sw": [], "hw": []}
        self.kind = {}

    def newbuf(self, dma=False):
        if dma:
            kind = dma if isinstance(dma, str) else "hw"
            if self.free_dsems[kind]:
                return Buf(self.free_dsems[kind].pop())
            s = Sem(self.nc, self.es, "d%d" % self.nsem)
            self.nsem += 1
            self.dsems.append(s)
            self.kind[s] = kind
            return Buf(s)
        return Buf()

    def release(self, sems):
        for sm in sems:
            self.free_dsems[self.kind[sm]].append(sm)

    def _waits(self, E, reads, writes):
        waits = {}

        def need(t):
            if t is None:
                return
            sm, v = t
            if sm is E.sem and E.name == "pe":
                return
            if waits.get(sm, 0) < v:
                waits[sm] = v

        for b in reads:
            need(b.lw)
        for b in writes:
            need(b.lw)
            for t in b.rd:
                need(t)
        wl = [(sm, v) for sm, v in waits.items() if E.waited.get(sm, 0) < v]
        for sm, v in wl:
            E.waited[sm] = v
        return wl

    def op(self, E, fn, reads=(), writes=()):
        wl = self._waits(E, reads, writes)
        E.sem.n += 1
        tick = (E.sem, E.sem.n)
        E.prog.append((fn, wl, (E.sem, 1)))
        for b in reads:
            b.rd.append(tick)
        for b in writes:
            b.lw = tick
            b.rd = []
        return tick

    def dma(self, Q, out, in_, slot, reads=(), writes=()):
        wl = self._waits(Q, reads, writes)
        sm = slot.dsem
        assert self.kind[sm] == ("sw" if Q is self.POOL else "hw"), "semaphore shared between SW and HW DGE"
        sm.n += 16
        tick = (sm, sm.n)
        Q.prog.append((lambda e, o=out, i=in_: e.dma_start(out=o, in_=i), wl, (sm, 16)))
        for b in reads:
            b.rd.append(tick)
        for b in writes:
            b.lw = tick
            b.rd = []
        return tick

    def barrier(self, dsems=()):
        engs = [self.PE, self.ACT, self.DVE, self.POOL, self.SP]
        targets = [(e.sem, e.sem.n) for e in engs if e.sem.n > 0]
        targets += [(s, s.n) for s in dsems if s.n > 0]
        for E in engs:
            wl = [(sm, v) for sm, v in targets if sm is not E.sem and E.waited.get(sm, 0) < v]
            for sm, v in wl:
                E.waited[sm] = v
            if wl:
                E.prog.append((None, wl, None))

    def finish(self):
        nc = self.nc

        def run(E, e):
            for fn, wl, inc in E.prog:
                for sm, v in wl:
                    e.wait_ge(sm.h, v)
                if fn is None:
                    continue
                ins = fn(e)
                if inc is not None:
                    ins.then_inc(inc[0].h, inc[1])

        with nc.Block() as blk:
            @blk.tensor
            def _(e):
                run(self.PE, e)

            @blk.scalar
            def _(e):
                run(self.ACT, e)

            @blk.vector
            def _(e):
                run(self.DVE, e)

            @blk.gpsimd
            def _(e):
                run(self.POOL, e)

            @blk.sync
            def _(e):
                run(self.SP, e)


class Rot:
    def __init__(self, items):
        self.items = items
        self.i = 0

    def next(self):
        it = self.items[self.i % len(self.items)]
        self.i += 1
        return it


def build_program(seg_types, TQ, seg_halo=None):
    NSEG = len(seg_types)
    if seg_halo is None:
        seg_halo = [True] * NSEG
    NTYPE = max(seg_types) + 1
    TK = TQ + 2 * HALO
    NT = TK // 512
    NQT = TQ // 512
    QT0 = HALO // 512

    nc = bass.Bass("TRN2", target_bir_lowering=False)

    def din(name, shape, dt=F32):
        return nc.dram_tensor(name, list(shape), dt, kind="ExternalInput").ap()

    xT_d = din("xT", [NSEG, D, TK])
    xn_d = din("xn", [NSEG, TQ, D])
    w_d = din("w_in_p", [D, NW])
    wba_d = din("w_br_a", [512, D])
    wbb_d = din("w_br_b", [512, D])
    wo_d = din("w_out", [D, D])
    cs_d = din("cs", [NTYPE, 128, 2, TK])
    valid_d = din("valid", [NTYPE, TK, 64])
    masks_d = din("masks", [128, 1280])
    hb_d = din("bgate", [128, 16])
    sink_d = din("sinkrep", [128, 4])
    zrows_d = din("zrows", [HALO, 512])
    lng_d = din("lng", [128, D])
    lnb_d = din("lnb", [128, D])
    y_d = nc.dram_tensor("y", [NSEG, TQ, D], F32, kind="ExternalOutput").ap()

    va_d = nc.dram_tensor("va_scr", [NSEG, TK, 576], BF16, kind="Internal").ap()
    vb_d = nc.dram_tensor("vb_scr", [NSEG, TK, 192], BF16, kind="Internal").ap()
    yg_d = nc.dram_tensor("yg_scr", [NSEG, 2, 512, TQ], BF16, kind="Internal").ap()
    ya_d = nc.dram_tensor("ya_scr", [NSEG, 2, 512, TQ], BF16, kind="Internal").ap()

    dscr_d = nc.dram_tensor("dscr", [2, 2, TQ], F32, kind="Internal").ap()
    es = ExitStack()
    P = Prog(nc, es)
    PE, ACT, DVE, POOL, SP = P.PE, P.ACT, P.DVE, P.POOL, P.SP

    SZ_W = 8 * 2048 * 2
    SZ_WBO = (4 + 4 + 8) * 1024 * 2
    SZ_QT = 4 * TQ * 2
    SZ_KT = 4 * TK * 2
    SZ_CONST = 4096
    SZ_SH = 57344
    OFF_W = 0
    OFF_WBO = OFF_W + SZ_W
    OFF_CONST = OFF_WBO + SZ_WBO
    OFF_QT = OFF_CONST + SZ_CONST
    OFF_KT = OFF_QT + SZ_QT
    OFF_SH = OFF_KT + SZ_KT
    TOTAL = OFF_SH + SZ_SH
    arena = es.enter_context(nc.sbuf_tensor("arena", [128, TOTAL // 2], BF16))
    arena32 = arena.bitcast(F32)
    psum = es.enter_context(nc.psum_tensor("psum", [128, 8, 512], F32))

    def v16(off, n):
        assert off % 2 == 0
        return arena[:, off // 2: off // 2 + n]

    def v32(off, n):
        assert off % 4 == 0
        return arena32[:, off // 4: off // 4 + n]

    W = v16(OFF_W, 8 * 2048).rearrange("p (k n) -> p k n", k=8)
    WBA = v16(OFF_WBO, 4 * 1024).rearrange("p (k n) -> p k n", k=4)
    WBB = v16(OFF_WBO + 8192, 4 * 1024).rearrange("p (k n) -> p k n", k=4)
    WO = v16(OFF_WBO + 16384, 8 * 1024).rearrange("p (k n) -> p k n", k=8)
    QT = v16(OFF_QT, 4 * TQ).rearrange("p (c t) -> p c t", c=4)
    KT = v16(OFF_KT, 4 * TK).rearrange("p (c t) -> p c t", c=4)
    MASK = v16(OFF_CONST, 768)
    MBA = MASK[:, 0:256]
    MBB = MASK[:, 256:640]
    IDENT = MASK[:, 640:768]
    HB = v32(OFF_CONST + 2048, 16)
    ES = v32(OFF_CONST + 2048 + 64, 4)
    SINK = v32(OFF_CONST + 2048 + 64 + 16, 4)
    MHALF = v32(OFF_CONST + 2048 + 128, 1)
    ST8 = v32(OFF_CONST + 3072, 64)
    DROW = v32(OFF_CONST + 3072 + 256, 2 * (TQ // 128)).rearrange("p (h f) -> p h f", h=2)

    b_W = P.newbuf(dma="sw")
    b_WBO = P.newbuf(dma="sw")
    b_const = P.newbuf(dma="hw")
    b_mask = P.newbuf(dma="sw")
    b_QT = [[Buf() for _ in range(NQT)] for _ in range(4)]
    b_KT = [[Buf() for _ in range(NT)] for _ in range(4)]
    banks = [Buf() for _ in range(8)]

    b_va = [[Buf() for _ in range(NT)] for _ in range(NSEG)]
    b_vb = [[Buf() for _ in range(NT)] for _ in range(NSEG)]
    b_vva = [P.newbuf(dma="sw") for _ in range(NSEG)]
    b_vvb = [P.newbuf(dma="sw") for _ in range(NSEG)]
    b_yg = [[[Buf() for _ in range(4)] for _ in range(2)] for _ in range(NSEG)]
    b_ya = [[[Buf() for _ in range(4)] for _ in range(2)] for _ in range(NSEG)]

    wv = w_d.rearrange("(k p) n -> p k n", p=128)

    def load_W(c0, ncols):
        for k in range(8):
            P.dma(POOL, W[:, k, 0:ncols], wv[:, k, c0:c0 + ncols], b_W, writes=[b_W])

    load_W(0, 2048)
    P.dma(POOL, MASK, masks_d[:, 0:768], b_mask, writes=[b_mask])
    P.dma(SP, HB, hb_d, b_const, writes=[b_const])
    P.dma(SP, SINK, sink_d, b_const, writes=[b_const])
    P.dma(POOL, WBA, wba_d.rearrange("(k p) n -> p k n", p=128), b_WBO, writes=[b_WBO])
    P.dma(POOL, WBB, wbb_d.rearrange("(k p) n -> p k n", p=128), b_WBO, writes=[b_WBO])
    for k in range(8):
        P.dma(POOL, WO[:, k, :], wo_d[k * 128:(k + 1) * 128, :], b_WBO, writes=[b_WBO])
    for s in range(NSEG):
        P.dma(POOL, va_d[s, :, 512:576], valid_d[seg_types[s]], b_vva[s], writes=[b_vva[s]])
        P.dma(POOL, vb_d[s, :, 128:192], valid_d[seg_types[s]], b_vvb[s], writes=[b_vvb[s]])
    halo_cts = [ct for ct in range(NT) if not (QT0 <= ct < QT0 + NQT)]
    for s in range(NSEG):
        if seg_halo[s]:
            continue
        for (lo, cts) in ((0, [ct for ct in halo_cts if ct < QT0]), (HALO + TQ, [ct for ct in halo_cts if ct >= QT0])):
            P.dma(POOL, va_d[s, lo:lo + HALO, 0:512], zrows_d, b_vva[s], writes=[b_va[s][ct] for ct in cts])
            P.dma(POOL, vb_d[s, lo:lo + HALO, 0:128], zrows_d[:, 0:128], b_vvb[s], writes=[b_vb[s][ct] for ct in cts])
    P.op(ACT, lambda e: e.activation(out=ES, in_=SINK, func=AF.Exp), reads=[b_const], writes=[b_const])
    P.op(DVE, lambda e: e.memset(MHALF, -0.5), writes=[b_const])
    P.op(DVE, lambda e: e.tensor_scalar(out=HB, in0=HB, scalar1=0.5, scalar2=None, op0=ALU.mult), reads=[b_const], writes=[b_const])

    def phase_proj(s, mixer):
        typ = seg_types[s]
        o = OFF_SH
        XS = [v16(o + i * 8192, 8 * 512).rearrange("p (k t) -> p k t", k=8) for i in range(2)]
        o += 16384
        TB = [v32(o + i * 4096, 1024).rearrange("p (a t) -> p a t", a=2) for i in range(2)]
        o += 8192
        TMPA = [v32(o + i * 2048, 512) for i in range(2)]
        o += 4096
        TMPB = [v32(o + i * 2048, 512) for i in range(2)]
        o += 4096
        STG = [v16(o + i * 1024, 512) for i in range(8)]
        o += 8192
        XF = [v32(o + i * 8192, 4 * 512).rearrange("p (k t) -> p k t", k=4) for i in range(2)]
        o += 16384
        assert o <= OFF_SH + SZ_SH
        xs_b = Rot([(XS[i], Buf()) for i in range(2)])
        xf_b = [(XF[i], P.newbuf(dma=True)) for i in range(2)]
        tb_b = Rot([(TB[i], P.newbuf(dma=True)) for i in range(2)])
        ta_b = Rot([(TMPA[i], Buf()) for i in range(2)])
        tbb_b = Rot([(TMPB[i], Buf()) for i in range(2)])
        stg_b = Rot([(STG[i], P.newbuf(dma=True)) for i in range(8)])
        bk = Rot([(psum[:, i, :], banks[i]) for i in range(8)])
        local_dsems = [b.dsem for _, b in xf_b + tb_b.items + stg_b.items]

        if mixer == 0:
            cq, ck, cv, cg = 0, 512, 1024, 1536
            nk, vw = 4, 512
            v_scr, bv = va_d, b_va
        else:
            cq, ck, cv, cg = 0, 512, 768, 896
            nk, vw = 2, 128
            v_scr, bv = vb_d, b_vb

        xv = xT_d[s].rearrange("(k p) t -> p k t", p=128)
        shuf = [i ^ 1 for i in range(32)]

        def xload(ct):
            for hh in range(2):
                xf, xfb = xf_b[hh]
                P.dma(SP, xf, xv[:, 4 * hh:4 * hh + 4, ct * 512:(ct + 1) * 512], xfb, writes=[xfb])

        tbl = {}

        def tload(ct):
            tb, tbuf = tb_b.next()
            P.dma(SP, tb, cs_d[typ, :, :, ct * 512:(ct + 1) * 512], tbuf, writes=[tbuf])
            tbl[ct] = (tb, tbuf)

        def do_cast():
            xs, xb = xs_b.next()
            for hh in range(2):
                xf, xfb = xf_b[hh]
                P.op(ACT, lambda e, xs=xs, xf=xf, hh=hh: e.activation(out=xs[:, 4 * hh:4 * hh + 4, :], in_=xf, func=AF.Copy), reads=[xfb], writes=[xb])
            return xs, xb

        if seg_halo[s]:
            cts = list(range(NT))
        else:
            cts = [ct for ct in range(NT) if QT0 <= ct < QT0 + NQT]
            for (lo, hc) in ((0, [ct for ct in halo_cts if ct < QT0]), (HALO + TQ, [ct for ct in halo_cts if ct >= QT0])):
                P.op(POOL, lambda e, lo=lo: e.memset(KT[:, 0:nk, lo:lo + HALO], 0.0), writes=[b_KT[c][ct] for c in range(nk) for ct in hc])
        xload(cts[0])
        tload(cts[0])
        nxt = do_cast()
        for ci_, ct in enumerate(cts):
            c0 = ct * 512
            isq = QT0 <= ct < QT0 + NQT
            xs, xb = nxt
            tb, tbuf = tbl.pop(ct)
            if ci_ + 1 < len(cts):
                xload(cts[ci_ + 1])
                tload(cts[ci_ + 1])

            def proj(col, xs=xs):
                ps, pb = bk.next()

                def f(e, ps=ps, col=col, xs=xs):
                    for k in range(8):
                        ins = e.matmul(ps, lhsT=W[:, k, col:col + 128], rhs=xs[:, k, :], start=(k == 0), stop=(k == 7))
                    return ins
                P.op(PE, f, reads=[b_W, xb], writes=[pb])
                return ps, pb

            def rope(ps, pb, dst, dbuf, tb=tb, tbuf=tbuf):
                ta, tab = ta_b.next()
                t2, t2b = tbb_b.next()
                P.op(DVE, lambda e: e.stream_shuffle(out=ta, in_=ps, mask=shuf), reads=[pb], writes=[tab])
                P.op(DVE, lambda e: e.tensor_tensor(out=t2, in0=ps, in1=tb[:, 0, :], op=ALU.mult), reads=[pb, tbuf], writes=[t2b])
                P.op(DVE, lambda e: e.tensor_tensor(out=ta, in0=ta, in1=tb[:, 1, :], op=ALU.mult), reads=[tab, tbuf], writes=[tab])
                P.op(DVE, lambda e: e.tensor_tensor(out=dst, in0=t2, in1=ta, op=ALU.add), reads=[tab, t2b], writes=[dbuf])

            for c in range(nk):
                ps, pb = proj(ck + c * 128)
                rope(ps, pb, KT[:, c, c0:c0 + 512], b_KT[c][ct])
            if isq:
                qi = ct - QT0
                q0 = qi * 512
                for c in range(4):
                    ps, pb = proj(cq + c * 128)
                    rope(ps, pb, QT[:, c, q0:q0 + 512], b_QT[c][qi])
                for c in range(4):
                    ps, pb = proj(cg + c * 128)
                    ta, tab = ta_b.next()
                    sg, sgb = stg_b.next()
                    P.op(ACT, lambda e, ta=ta, ps=ps: e.activation(out=ta, in_=ps, func=AF.Tanh, scale=0.5), reads=[pb], writes=[tab])
                    P.op(DVE, lambda e, ta=ta, ps=ps, sg=sg: e.scalar_tensor_tensor(out=sg, in0=ta, scalar=1.0, in1=ps, op0=ALU.add, op1=ALU.mult),
                         reads=[tab, pb], writes=[sgb])
                    P.dma(SP, yg_d[s, mixer, c * 128:(c + 1) * 128, q0:q0 + 512], sg, sgb, reads=[sgb], writes=[b_yg[s][mixer][c]])
            if ci_ + 1 < len(cts):
                nxt = do_cast()
            for tt in range(4):
                ps, pb = bk.next()

                def f(e, ps=ps, xs=xs, tt=tt):
                    for k in range(8):
                        ins = e.matmul(ps[:, 0:vw], lhsT=xs[:, k, tt * 128:(tt + 1) * 128], rhs=W[:, k, cv:cv + vw], start=(k == 0), stop=(k == 7))
                    return ins
                P.op(PE, f, reads=[b_W, xb], writes=[pb])
                sg, sgb = stg_b.next()
                P.op(ACT, lambda e, sg=sg, ps=ps: e.activation(out=sg[:, 0:vw], in_=ps[:, 0:vw], func=AF.Copy), reads=[pb], writes=[sgb])
                r0 = c0 + tt * 128
                P.dma(SP, v_scr[s, r0:r0 + 128, 0:vw], sg[:, 0:vw], sgb, reads=[sgb], writes=[bv[s][ct]])
        if mixer == 0:
            load_W(C_B, 1408)
        else:
            load_W(C_PRE, 2048)
        P.barrier(local_dsems)
        P.release(local_dsems)

    def phase_attn(s, mixer):
        o = OFF_SH
        ACCO = v32(o, TQ)
        o += TQ * 4
        ACCD = v32(o, TQ)
        o += TQ * 4
        SGP = v16(o, TQ)
        o += TQ * 2
        YP = SGP
        NPT = 4
        PT = [v16(o + i * 1536, 768).rearrange("p (h q) -> p h q", h=2) for i in range(NPT)]
        o += NPT * 1536
        vwid = 576 if mixer == 0 else 192
        NVT = 6
        VT = [v16(o + i * 1152, vwid) for i in range(NVT)]
        o += NVT * 1152
        assert o <= OFF_SH + SZ_SH, (o, OFF_SH + SZ_SH)
        ACH = 1024
        b_accs = [Buf() for _ in range(TQ // ACH)]
        b_rscr = Buf()
        b_fin = P.newbuf(dma="sw")
        b_mulm = P.newbuf(dma="sw")
        b_drows = [Buf(), Buf()]
        b_dscr = [Buf(), Buf()]
        pending = []

        def flush():
            while pending:
                pending.pop(0)()

        fronts = []

        def emit_fronts():
            for st in fronts:
                st["v"]()
            for st in fronts:
                st["qk"]()
            for st in fronts:
                st["mask"]()
                st["exp"]()
            flush()
            for st in fronts:
                pending.append(st["pv"])
                pending.extend(st["post"])
            del fronts[:]
        b_sgp = P.newbuf(dma=True)
        b_yp = b_sgp
        pt_b = Rot([(PT[i], Buf()) for i in range(NPT)])
        vt_b = Rot([(VT[i], P.newbuf(dma=True)) for i in range(NVT)])
        local_dsems = [b_sgp.dsem, b_fin.dsem, b_mulm.dsem] + [b.dsem for _, b in vt_b.items]
        scale = 1.0 / 8.0
        st_cnt = [0]
        bf_cnt = [0]

        if mixer == 0:
            v_scr, bv, bvv = va_d, b_va, b_vva[s]
            chains = []
            for d in GROUP_DILS:
                for r in range(d):
                    chains.append((d, r))
            roles = ((1, 1), (0, 0))
            ones_c0 = 512
        else:
            v_scr, bv, bvv = vb_d, b_vb, b_vvb[s]
            chains = [(1, 0)]
            roles = ((2, 1), (1, None), (0, 0))
            ones_c0 = 128
        maxdb = max(r[0] for r in roles)

        for pair in range(4):
            kc = pair if mixer == 0 else pair // 2
            first_chain = True
            for (d, r) in chains:
                NB = TQ // (128 * d)
                if mixer == 0:
                    kbase = r + HALO - 64 * d
                else:
                    kbase = HALO - 128
                for B0 in range(0, NB, 4):
                    nb = min(4, NB - B0)
                    j = bf_cnt[0] % 2
                    bf_cnt[0] += 1
                    OB, ob = psum[:, 2 + 2 * j, :], banks[2 + 2 * j]
                    DB, db_ = psum[:, 3 + 2 * j, :], banks[3 + 2 * j]
                    started = [False, False]
                    for t in range(B0, B0 + nb + maxdb):
                        served = []
                        for (dbk, role) in roles:
                            b = t - dbk
                            if B0 <= b < B0 + nb:
                                served.append((b - B0, role))
                        if not served:
                            continue
                        n = 128 * len(served)
                        rel0 = kbase + 128 * d * t
                        vt, vtb = vt_b.next()
                        tiles = set(range(rel0 // 512, (rel0 + 127 * d) // 512 + 1))
                        em_v = (lambda vt=vt, vtb=vtb, rel0=rel0, d=d, tiles=tiles:
                                P.dma(SP, vt, v_scr[s, rel0:rel0 + 127 * d + 1:d, :], vtb,
                                      reads=[bv[s][x] for x in tiles] + [bvv], writes=[vtb]))
                        half = st_cnt[0] % 2
                        st_cnt[0] += 1
                        pt, ptb = pt_b.next()
                        kreads = [b_KT[kc][x] for x in tiles]
                        qreads = []
                        for (cb, role) in served:
                            q0 = r + 128 * d * (B0 + cb)
                            for x in range(q0 // 512, (q0 + 127 * d) // 512 + 1):
                                if b_QT[pair][x] not in qreads:
                                    qreads.append(b_QT[pair][x])

                        sb0 = 0 if half == 0 else 6

                        if mixer == 0:
                            mcol = {1: 0, 0: 128}
                            MB = MBA
                        else:
                            mcol = {1: 0, None: 128, 0: 256}
                            MB = MBB
                        m0 = mcol[served[0][1]]
                        need_mask = any(role is not None for _, role in served)

                        def fqk(e, served=served, rel0=rel0, sb0=sb0, d=d, r=r, B0=B0, kc=kc, pair=pair, n=n, need_mask=need_mask):
                            ins = None
                            q0 = r + 128 * d * (B0 + served[0][0])
                            for h in range(2):
                                ins = e.matmul(psum[:, sb0 + h, 0:n],
                                               lhsT=KT[64 * h:64 * h + 64, kc, rel0:rel0 + 127 * d + 1:d],
                                               rhs=QT[64 * h:64 * h + 64, pair, q0:q0 + (n - 1) * d + 1:d],
                                               start=True, stop=not need_mask)
                            return ins

                        def fmask(e, sb0=sb0, n=n, m0=m0, MB=MB):
                            ins = None
                            for h in range(2):
                                ins = e.matmul(psum[:, sb0 + h, 0:n], lhsT=IDENT, rhs=MB[:, m0:m0 + n], start=False, stop=True,
                                               skip_group_check=True)
                            return ins
                        stb = [banks[sb0], banks[sb0 + 1]]
                        em_qk = (lambda fqk=fqk, rd=kreads + qreads, stb=stb: P.op(PE, fqk, reads=rd, writes=stb))
                        if need_mask:
                            em_mask = (lambda fmask=fmask, stb=stb: P.op(PE, fmask, reads=[b_mask], writes=stb))
                        else:
                            em_mask = (lambda: None)
                        em_exp = (lambda pt=pt, n=n, sb0=sb0, stb=stb, ptb=ptb:
                                  P.op(ACT, lambda e: e.activation(out=pt[:, :, 0:n], in_=psum[:, sb0:sb0 + 2, 0:n], func=AF.Exp, scale=scale),
                                       reads=stb, writes=[ptb]))

                        st_flags = [not started[0], not started[1]]
                        started[0] = started[1] = True

                        def fpv(e, served=served, pt=pt, vt=vt, OB=OB, DB=DB, pair=pair, st_flags=st_flags, n=n):
                            ins = None
                            c0 = served[0][0] * 128
                            for h in range(2):
                                if mixer == 0:
                                    vcol = (2 * pair + h) * 64
                                else:
                                    vcol = (pair // 2) * 64
                                e.matmul(OB[64 * h:64 * h + 64, c0:c0 + n], lhsT=vt[:, vcol:vcol + 64],
                                         rhs=pt[:, h, 0:n], start=st_flags[h], stop=False, skip_group_check=True)
                                ins = e.matmul(DB[64 * h:64 * h + 64, c0:c0 + n], lhsT=vt[:, ones_c0:ones_c0 + 64],
                                               rhs=pt[:, h, 0:n], start=st_flags[h], stop=False, skip_group_check=True)
                            return ins
                        fronts.append({"v": em_v, "qk": em_qk, "mask": em_mask, "exp": em_exp, "post": [],
                                       "pv": (lambda fpv=fpv, ptb=ptb, vtb=vtb, ob=ob, db_=db_: P.op(PE, fpv, reads=[ptb, vtb], writes=[ob, db_]))})
                        if len(fronts) >= ABATCH:
                            emit_fronts()
                    qs = r + 128 * d * B0
                    qe = qs + (128 * nb - 1) * d + 1
                    nn = 128 * nb
                    accb = [b_accs[x] for x in range(qs // ACH, (qe - 1) // ACH + 1)]

                    def evac(qs=qs, qe=qe, d=d, nn=nn, OB=OB, DB=DB, ob=ob, db_=db_, accb=accb, fc=first_chain, pair=pair):
                        if fc:
                            P.op(ACT, lambda e: e.activation(out=ACCO[:, qs:qe:d], in_=OB[:, 0:nn], func=AF.Copy), reads=[ob], writes=accb)
                            if mixer == 1:
                                P.op(DVE, lambda e: e.tensor_scalar(out=ACCD[:, qs:qe:d], in0=DB[:, 0:nn], scalar1=ES[:, pair:pair + 1], scalar2=None, op0=ALU.add),
                                     reads=[db_, b_const], writes=accb)
                            else:
                                P.op(DVE, lambda e: e.tensor_copy(out=ACCD[:, qs:qe:d], in_=DB[:, 0:nn]), reads=[db_], writes=accb)
                        else:
                            P.op(DVE, lambda e: e.tensor_tensor(out=ACCO[:, qs:qe:d], in0=OB[:, 0:nn], in1=ACCO[:, qs:qe:d], op=ALU.add),
                                 reads=[ob], writes=accb)
                            P.op(DVE, lambda e: e.tensor_tensor(out=ACCD[:, qs:qe:d], in0=DB[:, 0:nn], in1=ACCD[:, qs:qe:d], op=ALU.add),
                                 reads=[db_], writes=accb)
                    if fronts:
                        fronts[-1]["post"].append(evac)
                    else:
                        pending.append(evac)
                first_chain = False
            if fronts:
                emit_fronts()
            flush()
            P.dma(SP, SGP, yg_d[s, mixer, pair * 128:(pair + 1) * 128, :], b_sgp, reads=[b_yg[s][mixer][pair]], writes=[b_sgp])
            for h in range(2):
                P.dma(POOL, DROW[:, h, :], ACCD[64 * h:64 * h + 1, :], b_fin, reads=b_accs, writes=[b_drows[h]])
            P.op(DVE, lambda e: e.reciprocal(out=DROW, in_=DROW), reads=[], writes=b_drows)
            P.dma(POOL, dscr_d[1].rearrange("h (p f) -> p h f", p=128), DROW, b_fin, reads=b_drows, writes=[b_dscr[1]])
            for h in range(2):
                P.dma(POOL, ACCD[64 * h:64 * h + 64, :], dscr_d[1, h, :].partition_broadcast(64), b_fin, reads=[b_dscr[1]], writes=b_accs)
            CH = ACH
            for ci, c0 in enumerate(range(0, TQ, CH)):
                ab = [b_accs[ci]]
                P.op(DVE, lambda e, c0=c0: e.scalar_tensor_tensor(out=ACCO[:, c0:c0 + CH], in0=ACCO[:, c0:c0 + CH], scalar=0.5, in1=ACCD[:, c0:c0 + CH],
                                                                 op0=ALU.mult, op1=ALU.mult), reads=[], writes=ab)
                P.op(DVE, lambda e, c0=c0: e.tensor_tensor(out=YP[:, c0:c0 + CH], in0=ACCO[:, c0:c0 + CH], in1=SGP[:, c0:c0 + CH], op=ALU.mult),
                     reads=ab, writes=[b_sgp])
            P.dma(POOL, ya_d[s, mixer, pair * 128:(pair + 1) * 128, :], YP, b_fin, reads=[b_yp], writes=[b_ya[s][mixer][pair]])
        P.barrier(local_dsems)
        P.release(local_dsems)

    def phase_out(s):
        o = OFF_QT
        XS = [v16(o + i * 8192, 8 * 512).rearrange("p (k t) -> p k t", k=8) for i in range(2)]
        o += 16384
        YAB = [v16(o + i * 8192, 8 * 512).rearrange("p (k t) -> p k t", k=8) for i in range(2)]
        o += 16384
        G = v16(o, 16 * 512).rearrange("p (c t) -> p c t", c=16)
        o += 16384
        MG = v16(o, 8 * 512).rearrange("p (c t) -> p c t", c=8)
        o += 8192
        TMP = [v32(o + i * 2048, 512) for i in range(2)]
        o += 4096
        XR = [v32(o + i * 4096, 1024) for i in range(2)]
        o += 8192
        ZZ = [v32(o + i * 4096, 1024) for i in range(2)]
        o += 8192
        LNG = v32(o, 1024)
        o += 4096
        LNB = v32(o, 1024)
        o += 4096
        XF = [v32(o + i * 8192, 4 * 512).rearrange("p (k t) -> p k t", k=4) for i in range(2)]
        o += 16384
        assert o <= TOTAL
        xs_b = Rot([(XS[i], Buf()) for i in range(2)])
        xf_b = [(XF[i], P.newbuf(dma=True)) for i in range(2)]
        yab_b = Rot([(YAB[i], P.newbuf(dma=True)) for i in range(2)])
        tmp_b = Rot([(TMP[i], Buf()) for i in range(2)])
        xr_b = Rot([(XR[i], P.newbuf(dma=True)) for i in range(2)])
        zz_b = Rot([(ZZ[i], P.newbuf(dma="sw")) for i in range(2)])
        b_G = [Buf() for _ in range(16)]
        b_MG = [Buf() for _ in range(8)]
        b_ln = P.newbuf(dma=True)
        b_sts = [Buf(), Buf()]
        bk = Rot([(psum[:, i, :], banks[i]) for i in range(8)])
        local_dsems = [b.dsem for _, b in xf_b + yab_b.items + xr_b.items + zz_b.items] + [b_ln.dsem]

        P.dma(SP, LNG, lng_d, b_ln, writes=[b_ln])
        P.dma(SP, LNB, lnb_d, b_ln, writes=[b_ln])
        xv = xT_d[s].rearrange("(k p) t -> p k t", p=128)
        def xload(qi):
            c0 = HALO + qi * 512
            for hh in range(2):
                xf, xfb = xf_b[hh]
                P.dma(SP, xf, xv[:, 4 * hh:4 * hh + 4, c0:c0 + 512], xfb, writes=[xfb])

        ytl = {}

        def yload(qi):
            yab, yb = yab_b.next()
            for m in range(2):
                P.dma(SP, yab[:, 4 * m:4 * m + 4, :], ya_d[s, m].rearrange("(c p) t -> p c t", p=128)[:, :, qi * 512:(qi + 1) * 512], yb,
                      reads=b_ya[s][m], writes=[yb])
            ytl[qi] = (yab, yb)

        def do_cast():
            xs, xb = xs_b.next()
            for hh in range(2):
                xf, xfb = xf_b[hh]
                P.op(ACT, lambda e, xs=xs, xf=xf, hh=hh: e.activation(out=xs[:, 4 * hh:4 * hh + 4, :], in_=xf, func=AF.Copy), reads=[xfb], writes=[xb])
            return xs, xb

        xload(0)
        yload(0)
        nxt = do_cast()
        for qi in range(NQT):
            q0 = qi * 512
            xs, xb = nxt
            yab, yb = ytl.pop(qi)
            if qi + 1 < NQT:
                xload(qi + 1)
                yload(qi + 1)
            for cg in range(16):
                ps, pb = bk.next()

                def f(e, ps=ps, cg=cg, xs=xs):
                    for k in range(8):
                        ins = e.matmul(ps, lhsT=W[:, k, cg * 128:(cg + 1) * 128], rhs=xs[:, k, :], start=(k == 0), stop=(k == 7))
                    return ins
                P.op(PE, f, reads=[b_W, xb], writes=[pb])
                P.op(ACT, lambda e, ps=ps, cg=cg: e.activation(out=G[:, cg, :], in_=ps, func=AF.Tanh, scale=0.5, bias=HB[:, cg:cg + 1]),
                     reads=[pb, b_const], writes=[b_G[cg]])
            if qi + 1 < NQT:
                nxt = do_cast()
            for dc in range(8):
                psa, pba = bk.next()
                psb, pbb = bk.next()

                def fa(e, psa=psa, dc=dc, yab=yab):
                    for k in range(4):
                        ins = e.matmul(psa, lhsT=WBA[:, k, dc * 128:(dc + 1) * 128], rhs=yab[:, k, :], start=(k == 0), stop=(k == 3))
                    return ins

                def fb(e, psb=psb, dc=dc, yab=yab):
                    for k in range(4):
                        ins = e.matmul(psb, lhsT=WBB[:, k, dc * 128:(dc + 1) * 128], rhs=yab[:, 4 + k, :], start=(k == 0), stop=(k == 3))
                    return ins
                P.op(PE, fa, reads=[b_WBO, yb], writes=[pba])
                P.op(PE, fb, reads=[b_WBO, yb], writes=[pbb])
                t1, t1b = tmp_b.next()
                t2, t2b = tmp_b.next()
                P.op(DVE, lambda e, t1=t1, psa=psa, dc=dc: e.scalar_tensor_tensor(out=t1, in0=G[:, dc, :], scalar=1.0, in1=psa, op0=ALU.add, op1=ALU.mult),
                     reads=[pba, b_G[dc]], writes=[t1b])
                P.op(DVE, lambda e, t2=t2, psb=psb, dc=dc: e.scalar_tensor_tensor(out=t2, in0=G[:, 8 + dc, :], scalar=1.0, in1=psb, op0=ALU.add, op1=ALU.mult),
                     reads=[pbb, b_G[8 + dc]], writes=[t2b])
                P.op(DVE, lambda e, t1=t1, t2=t2, dc=dc: e.tensor_tensor(out=MG[:, dc, :], in0=t1, in1=t2, op=ALU.add),
                     reads=[t1b, t2b], writes=[b_MG[dc]])
            for tt in range(4):
                r0 = q0 + tt * 128
                sb = 32 * (tt % 2)
                bst = b_sts[tt % 2]
                xr, xrb = xr_b.next()
                P.dma(SP, xr, xn_d[s, r0:r0 + 128, :], xrb, writes=[xrb])
                zz, zzb = zz_b.next()
                P.op(ACT, lambda e, xr=xr: e.activation(out=xr, in_=xr, func=AF.Copy, scale=ALPHA), reads=[xrb], writes=[xrb])
                for hf in range(2):
                    ps, pb = bk.next()

                    def f(e, ps=ps, tt=tt, hf=hf):
                        for k in range(8):
                            ins = e.matmul(ps, lhsT=MG[:, k, tt * 128:(tt + 1) * 128], rhs=WO[:, k, hf * 512:(hf + 1) * 512], start=(k == 0), stop=(k == 7))
                        return ins
                    P.op(PE, f, reads=[b_WBO] + b_MG, writes=[pb])
                    P.op(DVE, lambda e, zz=zz, ps=ps, xr=xr, hf=hf: e.scalar_tensor_tensor(out=zz[:, hf * 512:(hf + 1) * 512], in0=ps, scalar=0.5,
                                                                                          in1=xr[:, hf * 512:(hf + 1) * 512], op0=ALU.mult, op1=ALU.add),
                         reads=[pb, xrb], writes=[zzb])
                    P.op(DVE, lambda e, zz=zz, hf=hf, sb=sb: e.bn_stats(out=ST8[:, sb + hf * 6:sb + (hf + 1) * 6], in_=zz[:, hf * 512:(hf + 1) * 512]),
                         reads=[zzb], writes=[bst])
                P.op(DVE, lambda e, sb=sb: e.bn_aggr(out=ST8[:, sb + 12:sb + 14], in_=ST8[:, sb + 0:sb + 12]), reads=[bst], writes=[bst])
                P.op(DVE, lambda e, sb=sb: e.tensor_scalar(out=ST8[:, sb + 14:sb + 15], in0=ST8[:, sb + 13:sb + 14], scalar1=LN_EPS, scalar2=None, op0=ALU.add), reads=[bst], writes=[bst])
                P.op(POOL, lambda e, sb=sb: e.tensor_tensor(out=ST8[:, sb + 15:sb + 16], in0=ST8[:, sb + 14:sb + 15], in1=MHALF, op=ALU.pow), reads=[bst, b_const], writes=[bst])
                P.op(DVE, lambda e, sb=sb: e.tensor_scalar(out=ST8[:, sb + 16:sb + 17], in0=ST8[:, sb + 12:sb + 13], scalar1=ST8[:, sb + 15:sb + 16], scalar2=-1.0, op0=ALU.mult, op1=ALU.mult),
                     reads=[bst], writes=[bst])
                P.op(ACT, lambda e, zz=zz, sb=sb: e.activation(out=zz, in_=zz, func=AF.Identity, scale=ST8[:, sb + 15:sb + 16], bias=ST8[:, sb + 16:sb + 17]),
                     reads=[zzb, bst], writes=[zzb])
                P.op(DVE, lambda e, zz=zz: e.tensor_tensor(out=zz, in0=zz, in1=LNG, op=ALU.mult), reads=[zzb, b_ln], writes=[zzb])
                P.op(POOL, lambda e, zz=zz: e.tensor_tensor(out=zz, in0=zz, in1=LNB, op=ALU.add), reads=[zzb, b_ln], writes=[zzb])
                P.dma(POOL, y_d[s, r0:r0 + 128, :], zz, zzb, reads=[zzb])
        if s + 1 < NSEG:
            load_W(0, 2048)
        P.barrier(local_dsems)
        P.release(local_dsems)

    for s in range(NSEG):
        phase_proj(s, 0)
        phase_attn(s, 0)
        phase_proj(s, 1)
        phase_attn(s, 1)
        phase_out(s)
    P.barrier(P.dsems)
    P.finish()
    es.close()
    return nc


def _perm_cols_interleave(w, nheads):
    idx = np.empty(64, np.int64)
    idx[0::2] = np.arange(32)
    idx[1::2] = 32 + np.arange(32)
    cols = np.concatenate([h * 64 + idx for h in range(nheads)])
    return w[:, cols]


def make_w_in_p(w_in):
    qa, ka, va, ga = w_in[:, 0:512], w_in[:, 512:1024], w_in[:, 1024:1536], w_in[:, 1536:2048]
    qb, kb, vb, gb = w_in[:, 2048:2560], w_in[:, 2560:2688], w_in[:, 2688:2816], w_in[:, 2816:3328]
    pre = w_in[:, 3328:5376]
    kbp = _perm_cols_interleave(kb, 2)
    kbdup = np.concatenate([kbp[:, 0:64], kbp[:, 0:64], kbp[:, 64:128], kbp[:, 64:128]], axis=1)
    out = np.concatenate([_perm_cols_interleave(qa, 8), _perm_cols_interleave(ka, 8), va, ga,
                          _perm_cols_interleave(qb, 8), kbdup, vb, gb, pre], axis=1)
    assert out.shape[1] == NW
    return np.ascontiguousarray(out, dtype=np.float32)


def make_tables(start, seq_len, TK):
    pos = np.arange(TK, dtype=np.int64) - HALO + start
    half = 32
    inv_freq = (np.float32(ROPE_THETA) ** (-np.arange(half, dtype=np.float32) / np.float32(half))).astype(np.float32)
    ang = pos.astype(np.float32)[None, :] * inv_freq[:, None]
    cos = np.cos(ang.astype(np.float64)).astype(np.float32)
    sin = np.sin(ang.astype(np.float64)).astype(np.float32)
    rows = np.arange(128)
    fi = (rows % 64) // 2
    sign = np.where(rows % 2 == 0, -1.0, 1.0).astype(np.float32)
    cs = np.empty((128, 2, TK), np.float32)
    cs[:, 0, :] = cos[fi]
    cs[:, 1, :] = sin[fi] * sign[:, None]
    valid = ((pos >= 0) & (pos < seq_len)).astype(np.float32)
    valid = np.repeat(valid[:, None], 64, axis=1)
    return cs, np.ascontiguousarray(valid)


def make_consts(b_gate, sink_logit, ln_gain, ln_bias):
    k = np.arange(128)[:, None]
    q = np.arange(128)[None, :]
    mU = (k >= q).astype(np.float32)
    mL = (k <= q).astype(np.float32)
    NEG = np.float32(-30000.0)
    bU = np.where(mU > 0, np.float32(0), NEG).astype(np.float32)
    bL = np.where(mL > 0, np.float32(0), NEG).astype(np.float32)
    masks = np.concatenate([bL, bU, bL, np.zeros((128, 128), np.float32), bU, np.eye(128, dtype=np.float32)], axis=1)
    masks = np.ascontiguousarray(np.concatenate([masks, mU, mU, mL, mL], axis=1))
    hb = np.ascontiguousarray(b_gate.reshape(16, 128).T.astype(np.float32))
    sinkrep = np.empty((128, 4), np.float32)
    for p in range(4):
        sinkrep[0:64, p] = sink_logit[2 * p]
        sinkrep[64:128, p] = sink_logit[2 * p + 1]
    lng = np.ascontiguousarray(np.broadcast_to(ln_gain[None, :], (128, D)), dtype=np.float32)
    lnb = np.ascontiguousarray(np.broadcast_to(ln_bias[None, :], (128, D)), dtype=np.float32)
    return masks, hb, sinkrep, lng, lnb


def seg_inputs(x_seq, start, TQ):
    S = x_seq.shape[0]
    TK = TQ + 2 * HALO
    xT = np.zeros((D, TK), np.float32)
    lo = max(0, start - HALO)
    hi = min(S, start + TQ + HALO)
    xT[:, lo - (start - HALO):hi - (start - HALO)] = x_seq[lo:hi].T
    return xT, np.ascontiguousarray(x_seq[start:start + TQ])


_NC_CACHE = {}


def kernel(x_prompt, x_sample, w_in, b_gate, sink_logit, w_branch_a, w_branch_b, w_out, ln_gain, ln_bias):
    x_prompt = np.asarray(x_prompt, np.float32)
    x_sample = np.asarray(x_sample, np.float32)
    TQ = 4096
    TK = TQ + 2 * HALO
    w_in_p = make_w_in_p(np.asarray(w_in, np.float32)[0])
    masks, hb, sinkrep, lng, lnb = make_consts(np.asarray(b_gate, np.float32)[0], np.asarray(sink_logit, np.float32)[0],
                                               np.asarray(ln_gain, np.float32)[0], np.asarray(ln_bias, np.float32)[0])
    cs_s, valid_s = make_tables(0, 4096, TK)
    key = ("full",)
    if key not in _NC_CACHE:
        _NC_CACHE[key] = build_program([0, 0, 1], TQ, seg_halo=[False, False, True])
    nc = _NC_CACHE[key]
    in_maps = []
    for c in range(NCORES):
        segs = [seg_inputs(x_sample[2 * c], 0, TQ), seg_inputs(x_sample[2 * c + 1], 0, TQ),
                seg_inputs(x_prompt[c // 2], (c % 2) * TQ, TQ)]
        cs_p, valid_p = make_tables((c % 2) * TQ, 8192, TK)
        in_maps.append({
            "xT": np.stack([sg[0] for sg in segs]),
            "xn": np.stack([sg[1] for sg in segs]),
            "w_in_p": w_in_p,
            "w_br_a": np.ascontiguousarray(np.asarray(w_branch_a, np.float32)[0]),
            "w_br_b": np.ascontiguousarray(np.asarray(w_branch_b, np.float32)[0]),
            "w_out": np.ascontiguousarray(np.asarray(w_out, np.float32)[0]),
            "cs": np.stack([cs_s, cs_p]),
            "valid": np.stack([valid_s, valid_p]),
            "zrows": np.zeros((HALO, 512), np.float32),
            "masks": masks, "bgate": hb, "sinkrep": sinkrep, "lng": lng, "lnb": lnb,
        })
    res = run_bass_kernel_spmd(nc, in_maps, core_ids=list(range(NCORES)))
    y_prompt = np.empty((4, 8192, D), np.float32)
    y_sample = np.empty((16, 4096, D), np.float32)
    for c in range(NCORES):
        y = res.results[c]["y"]
        y_sample[2 * c] = y[0]
        y_sample[2 * c + 1] = y[1]
        y_prompt[c // 2, (c % 2) * TQ:(c % 2 + 1) * TQ] = y[2]
    return (y_prompt, y_sample)
```

```python
import math
from contextlib import ExitStack

import numpy as np
import concourse.bass as bass
import concourse.mybir as mybir
from concourse.bass_utils import run_bass_kernel_spmd

F32 = mybir.dt.float32
BF16 = mybir.dt.bfloat16
AF = mybir.ActivationFunctionType
ALU = mybir.AluOpType

D = 1024
HALO = 1024
NCORES = 8
LN_EPS = 1e-5
ALPHA = 2.0 ** 0.25
ROPE_THETA = 10000.0
GROUP_DILS = (1, 4, 16)
ABATCH = 2

C_QA, C_KA, C_VA, C_GA = 0, 512, 1024, 1536
C_B = 2048
C_QB, C_KB, C_VB, C_GB = 2048, 2560, 2816, 2944
C_PRE = 3456
NW = C_PRE + 2048


class Sem:
    def __init__(self, nc, es, name):
        self.h = es.enter_context(nc.semaphore(name))
        self.n = 0


class Buf:
    __slots__ = ("lw", "rd", "dsem")

    def __init__(self, dsem=None):
        self.lw = None
        self.rd = []
        self.dsem = dsem


class Eng:
    def __init__(self, name, sem):
        self.name = name
        self.sem = sem
        self.prog = []
        self.waited = {}


class Prog:
    def __init__(self, nc, es):
        self.nc = nc
        self.es = es
        self.PE = Eng("pe", Sem(nc, es, "s_pe"))
        self.ACT = Eng("act", Sem(nc, es, "s_act"))
        self.DVE = Eng("dve", Sem(nc, es, "s_dve"))
        self.POOL = Eng("pool", Sem(nc, es, "s_pool"))
        self.SP = Eng("sp", Sem(nc, es, "s_sp"))
        self.nsem = 0
        self.dsems = []
        self.free_dsems = {"sw": [], "hw": []}
        self.kind = {}

    def newbuf(self, dma=False):
        if dma:
            kind = dma if isinstance(dma, str) else "hw"
            if self.free_dsems[kind]:
                return Buf(self.free_dsems[kind].pop())
            s = Sem(self.nc, self.es, "d%d" % self.nsem)
            self.nsem += 1
            self.dsems.append(s)
            self.kind[s] = kind
            return Buf(s)
        return Buf()

    def release(self, sems):
        for sm in sems:
            self.free_dsems[self.kind[sm]].append(sm)

    def _waits(self, E, reads, writes):
        waits = {}

        def need(t):
            if t is None:
                return
            sm, v = t
            if sm is E.sem and E.name == "pe":
                return
            if waits.get(sm, 0) < v:
                waits[sm] = v

        for b in reads:
            need(b.lw)
        for b in writes:
            need(b.lw)
            for t in b.rd:
                need(t)
        wl = [(sm, v) for sm, v in waits.items() if E.waited.get(sm, 0) < v]
        for sm, v in wl:
            E.waited[sm] = v
        return wl

    def op(self, E, fn, reads=(), writes=()):
        wl = self._waits(E, reads, writes)
        E.sem.n += 1
        tick = (E.sem, E.sem.n)
        E.prog.append((fn, wl, (E.sem, 1)))
        for b in reads:
            b.rd.append(tick)
        for b in writes:
            b.lw = tick
            b.rd = []
        return tick

    def dma(self, Q, out, in_, slot, reads=(), writes=()):
        wl = self._waits(Q, reads, writes)
        sm = slot.dsem
        assert self.kind[sm] == ("sw" if Q is self.POOL else "hw"), "semaphore shared between SW and HW DGE"
        sm.n += 16
        tick = (sm, sm.n)
        Q.prog.append((lambda e, o=out, i=in_: e.dma_start(out=o, in_=i), wl, (sm, 16)))
        for b in reads:
            b.rd.append(tick)
        for b in writes:
            b.lw = tick
            b.rd = []
        return tick

    def barrier(self, dsems=()):
        engs = [self.PE, self.ACT, self.DVE, self.POOL, self.SP]
        targets = [(e.sem, e.sem.n) for e in engs if e.sem.n > 0]
        targets += [(s, s.n) for s in dsems if s.n > 0]
        for E in engs:
            wl = [(sm, v) for sm, v in targets if sm is not E.sem and E.waited.get(sm, 0) < v]
            for sm, v in wl:
                E.waited[sm] = v
            if wl:
                E.prog.append((None, wl, None))

    def finish(self):
        nc = self.nc

        def run(E, e):
            for fn, wl, inc in E.prog:
                for sm, v in wl:
                    e.wait_ge(sm.h, v)
                if fn is None:
                    continue
                ins = fn(e)
                if inc is not None:
                    ins.then_inc(inc[0].h, inc[1])

        with nc.Block() as blk:
            @blk.tensor
            def _(e):
                run(self.PE, e)

            @blk.scalar
            def _(e):
                run(self.ACT, e)

            @blk.vector
            def _(e):
                run(self.DVE, e)

            @blk.gpsimd
            def _(e):
                run(self.POOL, e)

            @blk.sync
            def _(e):
                run(self.SP, e)


class Rot:
    def __init__(self, items):
        self.items = items
        self.i = 0

    def next(self):
        it = self.items[self.i % len(self.items)]
        self.i += 1
        return it


def build_program(seg_types, TQ, seg_halo=None):
    NSEG = len(seg_types)
    if seg_halo is None:
        seg_halo = [True] * NSEG
    NTYPE = max(seg_types) + 1
    TK = TQ + 2 * HALO
    NT = TK // 512
    NQT = TQ // 512
    QT0 = HALO // 512

    nc = bass.Bass("TRN2", target_bir_lowering=False)

    def din(name, shape, dt=F32):
        return nc.dram_tensor(name, list(shape), dt, kind="ExternalInput").ap()

    xT_d = din("xT", [NSEG, D, TK])
    xn_d = din("xn", [NSEG, TQ, D])
    w_d = din("w_in_p", [D, NW])
    wba_d = din("w_br_a", [512, D])
    wbb_d = din("w_br_b", [512, D])
    wo_d = din("w_out", [D, D])
    cs_d = din("cs", [NTYPE, 128, 2, TK])
    valid_d = din("valid", [NTYPE, TK, 64])
    masks_d = din("masks", [128, 1280])
    hb_d = din("bgate", [128, 16])
    sink_d = din("sinkrep", [128, 4])
    zrows_d = din("zrows", [HALO, 512])
    lng_d = din("lng", [128, D])
    lnb_d = din("lnb", [128, D])
    y_d = nc.dram_tensor("y", [NSEG, TQ, D], F32, kind="ExternalOutput").ap()

    va_d = nc.dram_tensor("va_scr", [NSEG, TK, 576], BF16, kind="Internal").ap()
    vb_d = nc.dram_tensor("vb_scr", [NSEG, TK, 192], BF16, kind="Internal").ap()
    yg_d = nc.dram_tensor("yg_scr", [NSEG, 2, 512, TQ], BF16, kind="Internal").ap()
    ya_d = nc.dram_tensor("ya_scr", [NSEG, 2, 512, TQ], BF16, kind="Internal").ap()

    dscr_d = nc.dram_tensor("dscr", [2, 2, TQ], F32, kind="Internal").ap()
    es = ExitStack()
    P = Prog(nc, es)
    PE, ACT, DVE, POOL, SP = P.PE, P.ACT, P.DVE, P.POOL, P.SP

    SZ_W = 8 * 2048 * 2
    SZ_WBO = (4 + 4 + 8) * 1024 * 2
    SZ_QT = 4 * TQ * 2
    SZ_KT = 4 * TK * 2
    SZ_CONST = 4096
    SZ_SH = 57344
    OFF_W = 0
    OFF_WBO = OFF_W + SZ_W
    OFF_CONST = OFF_WBO + SZ_WBO
    OFF_QT = OFF_CONST + SZ_CONST
    OFF_KT = OFF_QT + SZ_QT
    OFF_SH = OFF_KT + SZ_KT
    TOTAL = OFF_SH + SZ_SH
    arena = es.enter_context(nc.sbuf_tensor("arena", [128, TOTAL // 2], BF16))
    arena32 = arena.bitcast(F32)
    psum = es.enter_context(nc.psum_tensor("psum", [128, 8, 512], F32))

    def v16(off, n):
        assert off % 2 == 0
        return arena[:, off // 2: off // 2 + n]

    def v32(off, n):
        assert off % 4 == 0
        return arena32[:, off // 4: off // 4 + n]

    W = v16(OFF_W, 8 * 2048).rearrange("p (k n) -> p k n", k=8)
    WBA = v16(OFF_WBO, 4 * 1024).rearrange("p (k n) -> p k n", k=4)
    WBB = v16(OFF_WBO + 8192, 4 * 1024).rearrange("p (k n) -> p k n", k=4)
    WO = v16(OFF_WBO + 16384, 8 * 1024).rearrange("p (k n) -> p k n", k=8)
    QT = v16(OFF_QT, 4 * TQ).rearrange("p (c t) -> p c t", c=4)
    KT = v16(OFF_KT, 4 * TK).rearrange("p (c t) -> p c t", c=4)
    MASK = v16(OFF_CONST, 768)
    MBA = MASK[:, 0:256]
    MBB = MASK[:, 256:640]
    IDENT = MASK[:, 640:768]
    HB = v32(OFF_CONST + 2048, 16)
    ES = v32(OFF_CONST + 2048 + 64, 4)
    SINK = v32(OFF_CONST + 2048 + 64 + 16, 4)
    MHALF = v32(OFF_CONST + 2048 + 128, 1)
    ST8 = v32(OFF_CONST + 3072, 64)
    DROW = v32(OFF_CONST + 3072 + 256, 2 * (TQ // 128)).rearrange("p (h f) -> p h f", h=2)

    b_W = P.newbuf(dma="sw")
    b_WBO = P.newbuf(dma="sw")
    b_const = P.newbuf(dma="hw")
    b_mask = P.newbuf(dma="sw")
    b_QT = [[Buf() for _ in range(NQT)] for _ in range(4)]
    b_KT = [[Buf() for _ in range(NT)] for _ in range(4)]
    banks = [Buf() for _ in range(8)]

    b_va = [[Buf() for _ in range(NT)] for _ in range(NSEG)]
    b_vb = [[Buf() for _ in range(NT)] for _ in range(NSEG)]
    b_vva = [P.newbuf(dma="sw") for _ in range(NSEG)]
    b_vvb = [P.newbuf(dma="sw") for _ in range(NSEG)]
    b_yg = [[[Buf() for _ in range(4)] for _ in range(2)] for _ in range(NSEG)]
    b_ya = [[[Buf() for _ in range(4)] for _ in range(2)] for _ in range(NSEG)]

    wv = w_d.rearrange("(k p) n -> p k n", p=128)

    def load_W(c0, ncols):
        for k in range(8):
            P.dma(POOL, W[:, k, 0:ncols], wv[:, k, c0:c0 + ncols], b_W, writes=[b_W])

    load_W(0, 2048)
    P.dma(POOL, MASK, masks_d[:, 0:768], b_mask, writes=[b_mask])
    P.dma(SP, HB, hb_d, b_const, writes=[b_const])
    P.dma(SP, SINK, sink_d, b_const, writes=[b_const])
    P.dma(POOL, WBA, wba_d.rearrange("(k p) n -> p k n", p=128), b_WBO, writes=[b_WBO])
    P.dma(POOL, WBB, wbb_d.rearrange("(k p) n -> p k n", p=128), b_WBO, writes=[b_WBO])
    for k in range(8):
        P.dma(POOL, WO[:, k, :], wo_d[k * 128:(k + 1) * 128, :], b_WBO, writes=[b_WBO])
    for s in range(NSEG):
        P.dma(POOL, va_d[s, :, 512:576], valid_d[seg_types[s]], b_vva[s], writes=[b_vva[s]])
        P.dma(POOL, vb_d[s, :, 128:192], valid_d[seg_types[s]], b_vvb[s], writes=[b_vvb[s]])
    halo_cts = [ct for ct in range(NT) if not (QT0 <= ct < QT0 + NQT)]
    for s in range(NSEG):
        if seg_halo[s]:
            continue
        for (lo, cts) in ((0, [ct for ct in halo_cts if ct < QT0]), (HALO + TQ, [ct for ct in halo_cts if ct >= QT0])):
            P.dma(POOL, va_d[s, lo:lo + HALO, 0:512], zrows_d, b_vva[s], writes=[b_va[s][ct] for ct in cts])
            P.dma(POOL, vb_d[s, lo:lo + HALO, 0:128], zrows_d[:, 0:128], b_vvb[s], writes=[b_vb[s][ct] for ct in cts])
    P.op(ACT, lambda e: e.activation(out=ES, in_=SINK, func=AF.Exp), reads=[b_const], writes=[b_const])
    P.op(DVE, lambda e: e.memset(MHALF, -0.5), writes=[b_const])
    P.op(DVE, lambda e: e.tensor_scalar(out=HB, in0=HB, scalar1=0.5, scalar2=None, op0=ALU.mult), reads=[b_const], writes=[b_const])

    def phase_proj(s, mixer):
        typ = seg_types[s]
        o = OFF_SH
        XS = [v16(o + i * 8192, 8 * 512).rearrange("p (k t) -> p k t", k=8) for i in range(2)]
        o += 16384
        TB = [v32(o + i * 4096, 1024).rearrange("p (a t) -> p a t", a=2) for i in range(2)]
        o += 8192
        TMPA = [v32(o + i * 2048, 512) for i in range(2)]
        o += 4096
        TMPB = [v32(o + i * 2048, 512) for i in range(2)]
        o += 4096
        STG = [v16(o + i * 1024, 512) for i in range(8)]
        o += 8192
        XF = [v32(o + i * 8192, 4 * 512).rearrange("p (k t) -> p k t", k=4) for i in range(2)]
        o += 16384
        assert o <= OFF_SH + SZ_SH
        xs_b = Rot([(XS[i], Buf()) for i in range(2)])
        xf_b = [(XF[i], P.newbuf(dma=True)) for i in range(2)]
        tb_b = Rot([(TB[i], P.newbuf(dma=True)) for i in range(2)])
        ta_b = Rot([(TMPA[i], Buf()) for i in range(2)])
        tbb_b = Rot([(TMPB[i], Buf()) for i in range(2)])
        stg_b = Rot([(STG[i], P.newbuf(dma=True)) for i in range(8)])
        bk = Rot([(psum[:, i, :], banks[i]) for i in range(8)])
        local_dsems = [b.dsem for _, b in xf_b + tb_b.items + stg_b.items]

        if mixer == 0:
            cq, ck, cv, cg = 0, 512, 1024, 1536
            nk, vw = 4, 512
            v_scr, bv = va_d, b_va
        else:
            cq, ck, cv, cg = 0, 512, 768, 896
            nk, vw = 2, 128
            v_scr, bv = vb_d, b_vb

        xv = xT_d[s].rearrange("(k p) t -> p k t", p=128)
        shuf = [i ^ 1 for i in range(32)]

        def xload(ct):
            for hh in range(2):
                xf, xfb = xf_b[hh]
                P.dma(SP, xf, xv[:, 4 * hh:4 * hh + 4, ct * 512:(ct + 1) * 512], xfb, writes=[xfb])

        tbl = {}

        def tload(ct):
            tb, tbuf = tb_b.next()
            P.dma(SP, tb, cs_d[typ, :, :, ct * 512:(ct + 1) * 512], tbuf, writes=[tbuf])
            tbl[ct] = (tb, tbuf)

        def do_cast():
            xs, xb = xs_b.next()
            for hh in range(2):
                xf, xfb = xf_b[hh]
                P.op(ACT, lambda e, xs=xs, xf=xf, hh=hh: e.activation(out=xs[:, 4 * hh:4 * hh + 4, :], in_=xf, func=AF.Copy), reads=[xfb], writes=[xb])
            return xs, xb

        if seg_halo[s]:
            cts = list(range(NT))
        else:
            cts = [ct for ct in range(NT) if QT0 <= ct < QT0 + NQT]
            for (lo, hc) in ((0, [ct for ct in halo_cts if ct < QT0]), (HALO + TQ, [ct for ct in halo_cts if ct >= QT0])):
                P.op(POOL, lambda e, lo=lo: e.memset(KT[:, 0:nk, lo:lo + HALO], 0.0), writes=[b_KT[c][ct] for c in range(nk) for ct in hc])
        xload(cts[0])
        tload(cts[0])
        nxt = do_cast()
        for ci_, ct in enumerate(cts):
            c0 = ct * 512
            isq = QT0 <= ct < QT0 + NQT
            xs, xb = nxt
            tb, tbuf = tbl.pop(ct)
            if ci_ + 1 < len(cts):
                xload(cts[ci_ + 1])
                tload(cts[ci_ + 1])

            def proj(col, xs=xs):
                ps, pb = bk.next()

                def f(e, ps=ps, col=col, xs=xs):
                    for k in range(8):
                        ins = e.matmul(ps, lhsT=W[:, k, col:col + 128], rhs=xs[:, k, :], start=(k == 0), stop=(k == 7))
                    return ins
                P.op(PE, f, reads=[b_W, xb], writes=[pb])
                return ps, pb

            def rope(ps, pb, dst, dbuf, tb=tb, tbuf=tbuf):
                ta, tab = ta_b.next()
                t2, t2b = tbb_b.next()
                P.op(DVE, lambda e: e.stream_shuffle(out=ta, in_=ps, mask=shuf), reads=[pb], writes=[tab])
                P.op(DVE, lambda e: e.tensor_tensor(out=t2, in0=ps, in1=tb[:, 0, :], op=ALU.mult), reads=[pb, tbuf], writes=[t2b])
                P.op(DVE, lambda e: e.tensor_tensor(out=ta, in0=ta, in1=tb[:, 1, :], op=ALU.mult), reads=[tab, tbuf], writes=[tab])
                P.op(DVE, lambda e: e.tensor_tensor(out=dst, in0=t2, in1=ta, op=ALU.add), reads=[tab, t2b], writes=[dbuf])

            for c in range(nk):
                ps, pb = proj(ck + c * 128)
                rope(ps, pb, KT[:, c, c0:c0 + 512], b_KT[c][ct])
            if isq:
                qi = ct - QT0
                q0 = qi * 512
                for c in range(4):
                    ps, pb = proj(cq + c * 128)
                    rope(ps, pb, QT[:, c, q0:q0 + 512], b_QT[c][qi])
                for c in range(4):
                    ps, pb = proj(cg + c * 128)
                    ta, tab = ta_b.next()
                    sg, sgb = stg_b.next()
                    P.op(ACT, lambda e, ta=ta, ps=ps: e.activation(out=ta, in_=ps, func=AF.Tanh, scale=0.5), reads=[pb], writes=[tab])
                    P.op(DVE, lambda e, ta=ta, ps=ps, sg=sg: e.scalar_tensor_tensor(out=sg, in0=ta, scalar=1.0, in1=ps, op0=ALU.add, op1=ALU.mult),
                         reads=[tab, pb], writes=[sgb])
                    P.dma(SP, yg_d[s, mixer, c * 128:(c + 1) * 128, q0:q0 + 512], sg, sgb, reads=[sgb], writes=[b_yg[s][mixer][c]])
            if ci_ + 1 < len(cts):
                nxt = do_cast()
            for tt in range(4):
                ps, pb = bk.next()

                def f(e, ps=ps, xs=xs, tt=tt):
                    for k in range(8):
                        ins = e.matmul(ps[:, 0:vw], lhsT=xs[:, k, tt * 128:(tt + 1) * 128], rhs=W[:, k, cv:cv + vw], start=(k == 0), stop=(k == 7))
                    return ins
                P.op(PE, f, reads=[b_W, xb], writes=[pb])
                sg, sgb = stg_b.next()
                P.op(ACT, lambda e, sg=sg, ps=ps: e.activation(out=sg[:, 0:vw], in_=ps[:, 0:vw], func=AF.Copy), reads=[pb], writes=[sgb])
                r0 = c0 + tt * 128
                P.dma(SP, v_scr[s, r0:r0 + 128, 0:vw], sg[:, 0:vw], sgb, reads=[sgb], writes=[bv[s][ct]])
        if mixer == 0:
            load_W(C_B, 1408)
        else:
            load_W(C_PRE, 2048)
        P.barrier(local_dsems)
        P.release(local_dsems)

    def phase_attn(s, mixer):
        o = OFF_SH
        ACCO = v32(o, TQ)
        o += TQ * 4
        ACCD = v32(o, TQ)
        o += TQ * 4
        SGP = v16(o, TQ)
        o += TQ * 2
        YP = SGP
        NPT = 4
        PT = [v16(o + i * 1536, 768).rearrange("p (h q) -> p h q", h=2) for i in range(NPT)]
        o += NPT * 1536
        vwid = 576 if mixer == 0 else 192
        NVT = 6
        VT = [v16(o + i * 1152, vwid) for i in range(NVT)]
        o += NVT * 1152
        assert o <= OFF_SH + SZ_SH, (o, OFF_SH + SZ_SH)
        ACH = 1024
        b_accs = [Buf() for _ in range(TQ // ACH)]
        b_rscr = Buf()
        b_fin = P.newbuf(dma="sw")
        b_mulm = P.newbuf(dma="sw")
        b_drows = [Buf(), Buf()]
        b_dscr = [Buf(), Buf()]
        pending = []

        def flush():
            while pending:
                pending.pop(0)()

        fronts = []

        def emit_fronts():
            for st in fronts:
                st["v"]()
            for st in fronts:
                st["qk"]()
            for st in fronts:
                st["mask"]()
                st["exp"]()
            flush()
            for st in fronts:
                pending.append(st["pv"])
                pending.extend(st["post"])
            del fronts[:]
        b_sgp = P.newbuf(dma=True)
        b_yp = b_sgp
        pt_b = Rot([(PT[i], Buf()) for i in range(NPT)])
        vt_b = Rot([(VT[i], P.newbuf(dma=True)) for i in range(NVT)])
        local_dsems = [b_sgp.dsem, b_fin.dsem, b_mulm.dsem] + [b.dsem for _, b in vt_b.items]
        scale = 1.0 / 8.0
        st_cnt = [0]
        bf_cnt = [0]

        if mixer == 0:
            v_scr, bv, bvv = va_d, b_va, b_vva[s]
            chains = []
            for d in GROUP_DILS:
                for r in range(d):
                    chains.append((d, r))
            roles = ((1, 1), (0, 0))
            ones_c0 = 512
        else:
            v_scr, bv, bvv = vb_d, b_vb, b_vvb[s]
            chains = [(1, 0)]
            roles = ((2, 1), (1, None), (0, 0))
            ones_c0 = 128
        maxdb = max(r[0] for r in roles)

        for pair in range(4):
            kc = pair if mixer == 0 else pair // 2
            first_chain = True
            for (d, r) in chains:
                NB = TQ // (128 * d)
                if mixer == 0:
                    kbase = r + HALO - 64 * d
                else:
                    kbase = HALO - 128
                for B0 in range(0, NB, 4):
                    nb = min(4, NB - B0)
                    j = bf_cnt[0] % 2
                    bf_cnt[0] += 1
                    OB, ob = psum[:, 2 + 2 * j, :], banks[2 + 2 * j]
                    DB, db_ = psum[:, 3 + 2 * j, :], banks[3 + 2 * j]
                    started = [False, False]
                    for t in range(B0, B0 + nb + maxdb):
                        served = []
                        for (dbk, role) in roles:
                            b = t - dbk
                            if B0 <= b < B0 + nb:
                                served.append((b - B0, role))
                        if not served:
                            continue
                        n = 128 * len(served)
                        rel0 = kbase + 128 * d * t
                        vt, vtb = vt_b.next()
                        tiles = set(range(rel0 // 512, (rel0 + 127 * d) // 512 + 1))
                        em_v = (lambda vt=vt, vtb=vtb, rel0=rel0, d=d, tiles=tiles:
                                P.dma(SP, vt, v_scr[s, rel0:rel0 + 127 * d + 1:d, :], vtb,
                                      reads=[bv[s][x] for x in tiles] + [bvv], writes=[vtb]))
                        half = st_cnt[0] % 2
                        st_cnt[0] += 1
                        pt, ptb = pt_b.next()
                        kreads = [b_KT[kc][x] for x in tiles]
                        qreads = []
                        for (cb, role) in served:
                            q0 = r + 128 * d * (B0 + cb)
                            for x in range(q0 // 512, (q0 + 127 * d) // 512 + 1):
                                if b_QT[pair][x] not in qreads:
                                    qreads.append(b_QT[pair][x])

                        sb0 = 0 if half == 0 else 6

                        if mixer == 0:
                            mcol = {1: 0, 0: 128}
                            MB = MBA
                        else:
                            mcol = {1: 0, None: 128, 0: 256}
                            MB = MBB
                        m0 = mcol[served[0][1]]
                        need_mask = any(role is not None for _, role in served)

                        def fqk(e, served=served, rel0=rel0, sb0=sb0, d=d, r=r, B0=B0, kc=kc, pair=pair, n=n, need_mask=need_mask):
                            ins = None
                            q0 = r + 128 * d * (B0 + served[0][0])
                            for h in range(2):
                                ins = e.matmul(psum[:, sb0 + h, 0:n],
                                               lhsT=KT[64 * h:64 * h + 64, kc, rel0:rel0 + 127 * d + 1:d],
                                               rhs=QT[64 * h:64 * h + 64, pair, q0:q0 + (n - 1) * d + 1:d],
                                               start=True, stop=not need_mask)
                            return ins

                        def fmask(e, sb0=sb0, n=n, m0=m0, MB=MB):
                            ins = None
                            for h in range(2):
                                ins = e.matmul(psum[:, sb0 + h, 0:n], lhsT=IDENT, rhs=MB[:, m0:m0 + n], start=False, stop=True,
                                               skip_group_check=True)
                            return ins
                        stb = [banks[sb0], banks[sb0 + 1]]
                        em_qk = (lambda fqk=fqk, rd=kreads + qreads, stb=stb: P.op(PE, fqk, reads=rd, writes=stb))
                        if need_mask:
                            em_mask = (lambda fmask=fmask, stb=stb: P.op(PE, fmask, reads=[b_mask], writes=stb))
                        else:
                            em_mask = (lambda: None)
                        em_exp = (lambda pt=pt, n=n, sb0=sb0, stb=stb, ptb=ptb:
                                  P.op(ACT, lambda e: e.activation(out=pt[:, :, 0:n], in_=psum[:, sb0:sb0 + 2, 0:n], func=AF.Exp, scale=scale),
                                       reads=stb, writes=[ptb]))

                        st_flags = [not started[0], not started[1]]
                        started[0] = started[1] = True

                        def fpv(e, served=served, pt=pt, vt=vt, OB=OB, DB=DB, pair=pair, st_flags=st_flags, n=n):
                            ins = None
                            c0 = served[0][0] * 128
                            for h in range(2):
                                if mixer == 0:
                                    vcol = (2 * pair + h) * 64
                                else:
                                    vcol = (pair // 2) * 64
                                e.matmul(OB[64 * h:64 * h + 64, c0:c0 + n], lhsT=vt[:, vcol:vcol + 64],
                                         rhs=pt[:, h, 0:n], start=st_flags[h], stop=False, skip_group_check=True)
                                ins = e.matmul(DB[64 * h:64 * h + 64, c0:c0 + n], lhsT=vt[:, ones_c0:ones_c0 + 64],
                                               rhs=pt[:, h, 0:n], start=st_flags[h], stop=False, skip_group_check=True)
                            return ins
                        fronts.append({"v": em_v, "qk": em_qk, "mask": em_mask, "exp": em_exp, "post": [],
                                       "pv": (lambda fpv=fpv, ptb=ptb, vtb=vtb, ob=ob, db_=db_: P.op(PE, fpv, reads=[ptb, vtb], writes=[ob, db_]))})
                        if len(fronts) >= ABATCH:
                            emit_fronts()
                    qs = r + 128 * d * B0
                    qe = qs + (128 * nb - 1) * d + 1
                    nn = 128 * nb
                    accb = [b_accs[x] for x in range(qs // ACH, (qe - 1) // ACH + 1)]

                    def evac(qs=qs, qe=qe, d=d, nn=nn, OB=OB, DB=DB, ob=ob, db_=db_, accb=accb, fc=first_chain, pair=pair):
                        if fc:
                            P.op(ACT, lambda e: e.activation(out=ACCO[:, qs:qe:d], in_=OB[:, 0:nn], func=AF.Copy), reads=[ob], writes=accb)
                            if mixer == 1:
                                P.op(DVE, lambda e: e.tensor_scalar(out=ACCD[:, qs:qe:d], in0=DB[:, 0:nn], scalar1=ES[:, pair:pair + 1], scalar2=None, op0=ALU.add),
                                     reads=[db_, b_const], writes=accb)
                            else:
                                P.op(DVE, lambda e: e.tensor_copy(out=ACCD[:, qs:qe:d], in_=DB[:, 0:nn]), reads=[db_], writes=accb)
                        else:
                            P.op(DVE, lambda e: e.tensor_tensor(out=ACCO[:, qs:qe:d], in0=OB[:, 0:nn], in1=ACCO[:, qs:qe:d], op=ALU.add),
                                 reads=[ob], writes=accb)
                            P.op(DVE, lambda e: e.tensor_tensor(out=ACCD[:, qs:qe:d], in0=DB[:, 0:nn], in1=ACCD[:, qs:qe:d], op=ALU.add),
                                 reads=[db_], writes=accb)
                    if fronts:
                        fronts[-1]["post"].append(evac)
                    else:
                        pending.append(evac)
                first_chain = False
            if fronts:
                emit_fronts()
            flush()
            P.dma(SP, SGP, yg_d[s, mixer, pair * 128:(pair + 1) * 128, :], b_sgp, reads=[b_yg[s][mixer][pair]], writes=[b_sgp])
            for h in range(2):
                P.dma(POOL, DROW[:, h, :], ACCD[64 * h:64 * h + 1, :], b_fin, reads=b_accs, writes=[b_drows[h]])
            P.op(DVE, lambda e: e.reciprocal(out=DROW, in_=DROW), reads=[], writes=b_drows)
            P.dma(POOL, dscr_d[1].rearrange("h (p f) -> p h f", p=128), DROW, b_fin, reads=b_drows, writes=[b_dscr[1]])
            for h in range(2):
                P.dma(POOL, ACCD[64 * h:64 * h + 64, :], dscr_d[1, h, :].partition_broadcast(64), b_fin, reads=[b_dscr[1]], writes=b_accs)
            CH = ACH
            for ci, c0 in enumerate(range(0, TQ, CH)):
                ab = [b_accs[ci]]
                P.op(DVE, lambda e, c0=c0: e.scalar_tensor_tensor(out=ACCO[:, c0:c0 + CH], in0=ACCO[:, c0:c0 + CH], scalar=0.5, in1=ACCD[:, c0:c0 + CH],
                                                                 op0=ALU.mult, op1=ALU.mult), reads=[], writes=ab)
                P.op(DVE, lambda e, c0=c0: e.tensor_tensor(out=YP[:, c0:c0 + CH], in0=ACCO[:, c0:c0 + CH], in1=SGP[:, c0:c0 + CH], op=ALU.mult),
                     reads=ab, writes=[b_sgp])
            P.dma(POOL, ya_d[s, mixer, pair * 128:(pair + 1) * 128, :], YP, b_fin, reads=[b_yp], writes=[b_ya[s][mixer][pair]])
        P.barrier(local_dsems)
        P.release(local_dsems)

    def phase_out(s):
        o = OFF_QT
        XS = [v16(o + i * 8192, 8 * 512).rearrange("p (k t) -> p k t", k=8) for i in range(2)]
        o += 16384
        YAB = [v16(o + i * 8192, 8 * 512).rearrange("p (k t) -> p k t", k=8) for i in range(2)]
        o += 16384
        G = v16(o, 16 * 512).rearrange("p (c t) -> p c t", c=16)
        o += 16384
        MG = v16(o, 8 * 512).rearrange("p (c t) -> p c t", c=8)
        o += 8192
        TMP = [v32(o + i * 2048, 512) for i in range(2)]
        o += 4096
        XR = [v32(o + i * 4096, 1024) for i in range(2)]
        o += 8192
        ZZ = [v32(o + i * 4096, 1024) for i in range(2)]
        o += 8192
        LNG = v32(o, 1024)
        o += 4096
        LNB = v32(o, 1024)
        o += 4096
        XF = [v32(o + i * 8192, 4 * 512).rearrange("p (k t) -> p k t", k=4) for i in range(2)]
        o += 16384
        assert o <= TOTAL
        xs_b = Rot([(XS[i], Buf()) for i in range(2)])
        xf_b = [(XF[i], P.newbuf(dma=True)) for i in range(2)]
        yab_b = Rot([(YAB[i], P.newbuf(dma=True)) for i in range(2)])
        tmp_b = Rot([(TMP[i], Buf()) for i in range(2)])
        xr_b = Rot([(XR[i], P.newbuf(dma=True)) for i in range(2)])
        zz_b = Rot([(ZZ[i], P.newbuf(dma="sw")) for i in range(2)])
        b_G = [Buf() for _ in range(16)]
        b_MG = [Buf() for _ in range(8)]
        b_ln = P.newbuf(dma=True)
        b_sts = [Buf(), Buf()]
        bk = Rot([(psum[:, i, :], banks[i]) for i in range(8)])
        local_dsems = [b.dsem for _, b in xf_b + yab_b.items + xr_b.items + zz_b.items] + [b_ln.dsem]

        P.dma(SP, LNG, lng_d, b_ln, writes=[b_ln])
        P.dma(SP, LNB, lnb_d, b_ln, writes=[b_ln])
        xv = xT_d[s].rearrange("(k p) t -> p k t", p=128)
        def xload(qi):
            c0 = HALO + qi * 512
            for hh in range(2):
                xf, xfb = xf_b[hh]
                P.dma(SP, xf, xv[:, 4 * hh:4 * hh + 4, c0:c0 + 512], xfb, writes=[xfb])

        ytl = {}

        def yload(qi):
            yab, yb = yab_b.next()
            for m in range(2):
                P.dma(SP, yab[:, 4 * m:4 * m + 4, :], ya_d[s, m].rearrange("(c p) t -> p c t", p=128)[:, :, qi * 512:(qi + 1) * 512], yb,
                      reads=b_ya[s][m], writes=[yb])
            ytl[qi] = (yab, yb)

        def do_cast():
            xs, xb = xs_b.next()
            for hh in range(2):
                xf, xfb = xf_b[hh]
                P.op(ACT, lambda e, xs=xs, xf=xf, hh=hh: e.activation(out=xs[:, 4 * hh:4 * hh + 4, :], in_=xf, func=AF.Copy), reads=[xfb], writes=[xb])
            return xs, xb

        xload(0)
        yload(0)
        nxt = do_cast()
        for qi in range(NQT):
            q0 = qi * 512
            xs, xb = nxt
            yab, yb = ytl.pop(qi)
            if qi + 1 < NQT:
                xload(qi + 1)
                yload(qi + 1)
            for cg in range(16):
                ps, pb = bk.next()

                def f(e, ps=ps, cg=cg, xs=xs):
                    for k in range(8):
                        ins = e.matmul(ps, lhsT=W[:, k, cg * 128:(cg + 1) * 128], rhs=xs[:, k, :], start=(k == 0), stop=(k == 7))
                    return ins
                P.op(PE, f, reads=[b_W, xb], writes=[pb])
                P.op(ACT, lambda e, ps=ps, cg=cg: e.activation(out=G[:, cg, :], in_=ps, func=AF.Tanh, scale=0.5, bias=HB[:, cg:cg + 1]),
                     reads=[pb, b_const], writes=[b_G[cg]])
            if qi + 1 < NQT:
                nxt = do_cast()
            elif s + 1 < NSEG:
                load_W(0, 2048)
            for dc in range(8):
                psa, pba = bk.next()
                psb, pbb = bk.next()

                def fa(e, psa=psa, dc=dc, yab=yab):
                    for k in range(4):
                        ins = e.matmul(psa, lhsT=WBA[:, k, dc * 128:(dc + 1) * 128], rhs=yab[:, k, :], start=(k == 0), stop=(k == 3))
                    return ins

                def fb(e, psb=psb, dc=dc, yab=yab):
                    for k in range(4):
                        ins = e.matmul(psb, lhsT=WBB[:, k, dc * 128:(dc + 1) * 128], rhs=yab[:, 4 + k, :], start=(k == 0), stop=(k == 3))
                    return ins
                P.op(PE, fa, reads=[b_WBO, yb], writes=[pba])
                P.op(PE, fb, reads=[b_WBO, yb], writes=[pbb])
                t1, t1b = tmp_b.next()
                t2, t2b = tmp_b.next()
                P.op(DVE, lambda e, t1=t1, psa=psa, dc=dc: e.scalar_tensor_tensor(out=t1, in0=G[:, dc, :], scalar=1.0, in1=psa, op0=ALU.add, op1=ALU.mult),
                     reads=[pba, b_G[dc]], writes=[t1b])
                P.op(DVE, lambda e, t2=t2, psb=psb, dc=dc: e.scalar_tensor_tensor(out=t2, in0=G[:, 8 + dc, :], scalar=1.0, in1=psb, op0=ALU.add, op1=ALU.mult),
                     reads=[pbb, b_G[8 + dc]], writes=[t2b])
                P.op(DVE, lambda e, t1=t1, t2=t2, dc=dc: e.tensor_tensor(out=MG[:, dc, :], in0=t1, in1=t2, op=ALU.add),
                     reads=[t1b, t2b], writes=[b_MG[dc]])
            for tt in range(4):
                r0 = q0 + tt * 128
                sb = 32 * (tt % 2)
                bst = b_sts[tt % 2]
                xr, xrb = xr_b.next()
                P.dma(SP, xr, xn_d[s, r0:r0 + 128, :], xrb, writes=[xrb])
                zz, zzb = zz_b.next()
                P.op(ACT, lambda e, xr=xr: e.activation(out=xr, in_=xr, func=AF.Copy, scale=ALPHA), reads=[xrb], writes=[xrb])
                for hf in range(2):
                    ps, pb = bk.next()

                    def f(e, ps=ps, tt=tt, hf=hf):
                        for k in range(8):
                            ins = e.matmul(ps, lhsT=MG[:, k, tt * 128:(tt + 1) * 128], rhs=WO[:, k, hf * 512:(hf + 1) * 512], start=(k == 0), stop=(k == 7))
                        return ins
                    P.op(PE, f, reads=[b_WBO] + b_MG, writes=[pb])
                    P.op(DVE, lambda e, zz=zz, ps=ps, xr=xr, hf=hf: e.scalar_tensor_tensor(out=zz[:, hf * 512:(hf + 1) * 512], in0=ps, scalar=0.5,
                                                                                          in1=xr[:, hf * 512:(hf + 1) * 512], op0=ALU.mult, op1=ALU.add),
                         reads=[pb, xrb], writes=[zzb])
                    P.op(DVE, lambda e, zz=zz, hf=hf, sb=sb: e.bn_stats(out=ST8[:, sb + hf * 6:sb + (hf + 1) * 6], in_=zz[:, hf * 512:(hf + 1) * 512]),
                         reads=[zzb], writes=[bst])
                P.op(DVE, lambda e, sb=sb: e.bn_aggr(out=ST8[:, sb + 12:sb + 14], in_=ST8[:, sb + 0:sb + 12]), reads=[bst], writes=[bst])
                P.op(DVE, lambda e, sb=sb: e.tensor_scalar(out=ST8[:, sb + 14:sb + 15], in0=ST8[:, sb + 13:sb + 14], scalar1=LN_EPS, scalar2=None, op0=ALU.add), reads=[bst], writes=[bst])
                P.op(POOL, lambda e, sb=sb: e.tensor_tensor(out=ST8[:, sb + 15:sb + 16], in0=ST8[:, sb + 14:sb + 15], in1=MHALF, op=ALU.pow), reads=[bst, b_const], writes=[bst])
                P.op(DVE, lambda e, sb=sb: e.tensor_scalar(out=ST8[:, sb + 16:sb + 17], in0=ST8[:, sb + 12:sb + 13], scalar1=ST8[:, sb + 15:sb + 16], scalar2=-1.0, op0=ALU.mult, op1=ALU.mult),
                     reads=[bst], writes=[bst])
                P.op(ACT, lambda e, zz=zz, sb=sb: e.activation(out=zz, in_=zz, func=AF.Identity, scale=ST8[:, sb + 15:sb + 16], bias=ST8[:, sb + 16:sb + 17]),
                     reads=[zzb, bst], writes=[zzb])
                P.op(DVE, lambda e, zz=zz: e.tensor_tensor(out=zz, in0=zz, in1=LNG, op=ALU.mult), reads=[zzb, b_ln], writes=[zzb])
                P.op(POOL, lambda e, zz=zz: e.tensor_tensor(out=zz, in0=zz, in1=LNB, op=ALU.add), reads=[zzb, b_ln], writes=[zzb])
                P.dma(POOL, y_d[s, r0:r0 + 128, :], zz, zzb, reads=[zzb])
        P.barrier(local_dsems)
        P.release(local_dsems)

    for s in range(NSEG):
        phase_proj(s, 0)
        phase_attn(s, 0)
        phase_proj(s, 1)
        phase_attn(s, 1)
        phase_out(s)
    P.barrier(P.dsems)
    P.finish()
    es.close()
    return nc


def _perm_cols_interleave(w, nheads):
    idx = np.empty(64, np.int64)
    idx[0::2] = np.arange(32)
    idx[1::2] = 32 + np.arange(32)
    cols = np.concatenate([h * 64 + idx for h in range(nheads)])
    return w[:, cols]


def make_w_in_p(w_in):
    qa, ka, va, ga = w_in[:, 0:512], w_in[:, 512:1024], w_in[:, 1024:1536], w_in[:, 1536:2048]
    qb, kb, vb, gb = w_in[:, 2048:2560], w_in[:, 2560:2688], w_in[:, 2688:2816], w_in[:, 2816:3328]
    pre = w_in[:, 3328:5376]
    kbp = _perm_cols_interleave(kb, 2)
    kbdup = np.concatenate([kbp[:, 0:64], kbp[:, 0:64], kbp[:, 64:128], kbp[:, 64:128]], axis=1)
    out = np.concatenate([_perm_cols_interleave(qa, 8), _perm_cols_interleave(ka, 8), va, ga,
                          _perm_cols_interleave(qb, 8), kbdup, vb, gb, pre], axis=1)
    assert out.shape[1] == NW
    return np.ascontiguousarray(out, dtype=np.float32)


def make_tables(start, seq_len, TK):
    pos = np.arange(TK, dtype=np.int64) - HALO + start
    half = 32
    inv_freq = (np.float32(ROPE_THETA) ** (-np.arange(half, dtype=np.float32) / np.float32(half))).astype(np.float32)
    ang = pos.astype(np.float32)[None, :] * inv_freq[:, None]
    cos = np.cos(ang.astype(np.float64)).astype(np.float32)
    sin = np.sin(ang.astype(np.float64)).astype(np.float32)
    rows = np.arange(128)
    fi = (rows % 64) // 2
    sign = np.where(rows % 2 == 0, -1.0, 1.0).astype(np.float32)
    cs = np.empty((128, 2, TK), np.float32)
    cs[:, 0, :] = cos[fi]
    cs[:, 1, :] = sin[fi] * sign[:, None]
    valid = ((pos >= 0) & (pos < seq_len)).astype(np.float32)
    valid = np.repeat(valid[:, None], 64, axis=1)
    return cs, np.ascontiguousarray(valid)


def make_consts(b_gate, sink_logit, ln_gain, ln_bias):
    k = np.arange(128)[:, None]
    q = np.arange(128)[None, :]
    mU = (k >= q).astype(np.float32)
    mL = (k <= q).astype(np.float32)
    NEG = np.float32(-30000.0)
    bU = np.where(mU > 0, np.float32(0), NEG).astype(np.float32)
    bL = np.where(mL > 0, np.float32(0), NEG).astype(np.float32)
    masks = np.concatenate([bL, bU, bL, np.zeros((128, 128), np.float32), bU, np.eye(128, dtype=np.float32)], axis=1)
    masks = np.ascontiguousarray(np.concatenate([masks, mU, mU, mL, mL], axis=1))
    hb = np.ascontiguousarray(b_gate.reshape(16, 128).T.astype(np.float32))
    sinkrep = np.empty((128, 4), np.float32)
    for p in range(4):
        sinkrep[0:64, p] = sink_logit[2 * p]
        sinkrep[64:128, p] = sink_logit[2 * p + 1]
    lng = np.ascontiguousarray(np.broadcast_to(ln_gain[None, :], (128, D)), dtype=np.float32)
    lnb = np.ascontiguousarray(np.broadcast_to(ln_bias[None, :], (128, D)), dtype=np.float32)
    return masks, hb, sinkrep, lng, lnb


def seg_inputs(x_seq, start, TQ):
    S = x_seq.shape[0]
    TK = TQ + 2 * HALO
    xT = np.zeros((D, TK), np.float32)
    lo = max(0, start - HALO)
    hi = min(S, start + TQ + HALO)
    xT[:, lo - (start - HALO):hi - (start - HALO)] = x_seq[lo:hi].T
    return xT, np.ascontiguousarray(x_seq[start:start + TQ])


_NC_CACHE = {}


def kernel(x_prompt, x_sample, w_in, b_gate, sink_logit, w_branch_a, w_branch_b, w_out, ln_gain, ln_bias):
    x_prompt = np.asarray(x_prompt, np.float32)
    x_sample = np.asarray(x_sample, np.float32)
    TQ = 4096
    TK = TQ + 2 * HALO
    w_in_p = make_w_in_p(np.asarray(w_in, np.float32)[0])
    masks, hb, sinkrep, lng, lnb = make_consts(np.asarray(b_gate, np.float32)[0], np.asarray(sink_logit, np.float32)[0],
                                               np.asarray(ln_gain, np.float32)[0], np.asarray(ln_bias, np.float32)[0])
    cs_s, valid_s = make_tables(0, 4096, TK)
    key = ("full",)
    if key not in _NC_CACHE:
        _NC_CACHE[key] = build_program([0, 0, 1], TQ, seg_halo=[False, False, True])
    nc = _NC_CACHE[key]
    in_maps = []
    for c in range(NCORES):
        segs = [seg_inputs(x_sample[2 * c], 0, TQ), seg_inputs(x_sample[2 * c + 1], 0, TQ),
                seg_inputs(x_prompt[c // 2], (c % 2) * TQ, TQ)]
        cs_p, valid_p = make_tables((c % 2) * TQ, 8192, TK)
        in_maps.append({
            "xT": np.stack([sg[0] for sg in segs]),
            "xn": np.stack([sg[1] for sg in segs]),
            "w_in_p": w_in_p,
            "w_br_a": np.ascontiguousarray(np.asarray(w_branch_a, np.float32)[0]),
            "w_br_b": np.ascontiguousarray(np.asarray(w_branch_b, np.float32)[0]),
            "w_out": np.ascontiguousarray(np.asarray(w_out, np.float32)[0]),
            "cs": np.stack([cs_s, cs_p]),
            "valid": np.stack([valid_s, valid_p]),
            "zrows": np.zeros((HALO, 512), np.float32),
            "masks": masks, "bgate": hb, "sinkrep": sinkrep, "lng": lng, "lnb": lnb,
        })
    res = run_bass_kernel_spmd(nc, in_maps, core_ids=list(range(NCORES)))
    y_prompt = np.empty((4, 8192, D), np.float32)
    y_sample = np.empty((16, 4096, D), np.float32)
    for c in range(NCORES):
        y = res.results[c]["y"]
        y_sample[2 * c] = y[0]
        y_sample[2 * c + 1] = y[1]
        y_prompt[c // 2, (c % 2) * TQ:(c % 2 + 1) * TQ] = y[2]
    return (y_prompt, y_sample)
```
